# Optimizing a Trainium2 kernel written in Bass

```python
import jax, jax.numpy as jnp
from jax import lax
import numpy as np

D_MODEL = 1024
BATCH = 8
SEQ = 4096
DEPTH = 4

N_MIXERS = 2
EXPAND = 2
D_INNER = EXPAND * D_MODEL
A_CHUNK = 128
A_GROUPS = 8
A_GROUP_DIM = D_INNER // A_GROUPS
B_HEADDIM = 64
B_HEADS = D_INNER // B_HEADDIM
B_GROUPS = 8
B_HEADS_PER_GROUP = B_HEADS // B_GROUPS
B_STATE = 128
B_CONV = 4
B_CHUNK = 128
B_GN = B_GROUPS * B_STATE
B_CONV_DIM = D_INNER + 2 * B_GN
B_IN_DIM = D_INNER + B_CONV_DIM + B_HEADS
N_A_LAYERS = (DEPTH + 1) // N_MIXERS
N_B_LAYERS = DEPTH // N_MIXERS
NORM_EPS = 1e-6
LN_EPS = 1e-5

kernel_name = "hybrid_sgu_ssd_interleaved"


def rmsnorm(x, w):
    xf = x.astype(jnp.float32)
    y = xf * lax.rsqrt(jnp.mean(xf * xf, axis=-1, keepdims=True) + NORM_EPS)
    return (y * w.astype(jnp.float32)).astype(x.dtype)


def layernorm(x, w, b):
    xf = x.astype(jnp.float32)
    mu = jnp.mean(xf, axis=-1, keepdims=True)
    var = jnp.mean(jnp.square(xf - mu), axis=-1, keepdims=True)
    y = (xf - mu) * lax.rsqrt(var + LN_EPS)
    return (y * w.astype(jnp.float32) + b.astype(jnp.float32)).astype(x.dtype)


def spatial_gating_mixer(h, w_in, ln_w, ln_b, w_s, b_s, w_out):
    bsz, L, _ = h.shape
    z, u, v = jnp.split(h @ w_in, 3, axis=-1)
    v = layernorm(v, ln_w, ln_b)
    nc = L // A_CHUNK
    v = v.reshape(bsz, nc, A_CHUNK, A_GROUPS, A_GROUP_DIM)
    causal = jnp.tril(jnp.ones((A_CHUNK, A_CHUNK), dtype=bool))
    w_causal = jnp.where(causal[None], w_s, jnp.zeros((), w_s.dtype))
    mixed = jnp.einsum('gts,bcsgd->bctgd', w_causal, v) + b_s.T[None, None, :, :, None]
    mixed = mixed.reshape(bsz, L, D_INNER)
    return (u * mixed * jax.nn.silu(z)) @ w_out


def causal_depthwise_conv(x, w, b):
    y = lax.conv_general_dilated(
        x, w[:, None, :].astype(x.dtype), window_strides=(1,), padding=[(B_CONV - 1, 0)],
        dimension_numbers=('NWC', 'WIO', 'NWC'), feature_group_count=x.shape[-1])
    return y + b


def ssd_mixer(h, w_in, conv_w, conv_b, dt_bias, a_log, d_skip, norm_w, w_out):
    bsz, L, _ = h.shape
    G, J, P, N = B_GROUPS, B_HEADS_PER_GROUP, B_HEADDIM, B_STATE
    proj = h @ w_in
    z = proj[..., :D_INNER]
    xbc = proj[..., D_INNER:D_INNER + B_CONV_DIM]
    dt_raw = proj[..., D_INNER + B_CONV_DIM:]
    xbc = jax.nn.silu(causal_depthwise_conv(xbc, conv_w, conv_b))
    nc = L // B_CHUNK
    x = xbc[..., :D_INNER].reshape(bsz, nc, B_CHUNK, G, J, P)
    Bm = xbc[..., D_INNER:D_INNER + B_GN].reshape(bsz, nc, B_CHUNK, G, N)
    Cm = xbc[..., D_INNER + B_GN:].reshape(bsz, nc, B_CHUNK, G, N)
    dt = jax.nn.softplus(dt_raw.astype(jnp.float32) + dt_bias.astype(jnp.float32))
    dt = dt.reshape(bsz, nc, B_CHUNK, G, J)
    A = -jnp.exp(a_log.astype(jnp.float32)).reshape(G, J)
    dA_cs = jnp.cumsum(dt * A, axis=2)

    seg = dA_cs[:, :, :, None] - dA_cs[:, :, None, :]
    causal = jnp.tril(jnp.ones((B_CHUNK, B_CHUNK), dtype=bool))[:, :, None, None]
    decay = jnp.exp(jnp.where(causal, seg, -jnp.inf))
    cb = jnp.einsum('bclgn,bcsgn->bclsg', Cm, Bm)
    w_ls = (cb[..., None].astype(jnp.float32) * decay * dt[:, :, None]).astype(x.dtype)
    y_diag = jnp.einsum('bclsgj,bcsgjp->bclgjp', w_ls, x)

    decay_to_end = jnp.exp(dA_cs[:, :, -1:] - dA_cs)
    states = jnp.einsum('bclgn,bclgj,bclgjp->bcgjpn', Bm, (decay_to_end * dt).astype(x.dtype), x)
    chunk_decay = jnp.exp(dA_cs[:, :, -1])

    def step(carry, inp):
        st, dec = inp
        return carry * dec[..., None, None] + st, carry

    init = jnp.zeros((bsz, G, J, P, N), jnp.float32)
    _, prev = lax.scan(step, init, (jnp.moveaxis(states.astype(jnp.float32), 1, 0),
                                    jnp.moveaxis(chunk_decay, 1, 0)))
    prev = jnp.moveaxis(prev, 0, 1)
    y_off = jnp.einsum('bclgn,bcgjpn,bclgj->bclgjp', Cm.astype(jnp.float32), prev, jnp.exp(dA_cs))

    y = y_diag + y_off.astype(x.dtype) + x * d_skip.reshape(G, J)[..., None].astype(x.dtype)
    y = y.reshape(bsz, L, D_INNER)
    gated = (y * jax.nn.silu(z)).reshape(bsz, L, B_GROUPS, D_INNER // B_GROUPS)
    gated = rmsnorm(gated, norm_w.reshape(B_GROUPS, D_INNER // B_GROUPS)).reshape(bsz, L, D_INNER)
    return gated @ w_out


def setup_inputs(seed: int = 0) -> dict:
    key = jax.random.key(seed)
    ks = jax.random.split(key, 20)
    f32 = jnp.float32
    nA, nB = N_A_LAYERS, N_B_LAYERS
    x = jax.random.normal(ks[0], (BATCH, SEQ, D_MODEL), f32)
    norm_w = 1.0 + 0.02 * jax.random.normal(ks[1], (DEPTH, D_MODEL), f32)
    final_norm_w = 1.0 + 0.02 * jax.random.normal(ks[2], (D_MODEL,), f32)
    a_w_in = jax.random.normal(ks[3], (nA, D_MODEL, 3 * D_INNER), f32) * D_MODEL ** -0.5
    a_ln_w = 1.0 + 0.02 * jax.random.normal(ks[4], (nA, D_INNER), f32)
    a_ln_b = 0.02 * jax.random.normal(ks[5], (nA, D_INNER), f32)
    a_w_s = jax.random.normal(ks[6], (nA, A_GROUPS, A_CHUNK, A_CHUNK), f32) * A_CHUNK ** -0.5
    a_b_s = 1.0 + 0.1 * jax.random.normal(ks[7], (nA, A_GROUPS, A_CHUNK), f32)
    a_w_out = jax.random.normal(ks[8], (nA, D_INNER, D_MODEL), f32) * D_INNER ** -0.5
    b_w_in = jax.random.normal(ks[9], (nB, D_MODEL, B_IN_DIM), f32) * D_MODEL ** -0.5
    b_conv_w = jax.random.normal(ks[10], (nB, B_CONV, B_CONV_DIM), f32) * B_CONV ** -0.5
    b_conv_b = 0.02 * jax.random.normal(ks[11], (nB, B_CONV_DIM), f32)
    dt0 = jnp.exp(jax.random.uniform(ks[12], (nB, B_HEADS), f32, np.log(1e-3), np.log(1e-1)))
    b_dt_bias = dt0 + jnp.log(-jnp.expm1(-dt0))
    b_a_log = jnp.log(jax.random.uniform(ks[13], (nB, B_HEADS), f32, 1.0, 16.0))
    b_d_skip = 1.0 + 0.1 * jax.random.normal(ks[14], (nB, B_HEADS), f32)
    b_norm_w = 1.0 + 0.02 * jax.random.normal(ks[15], (nB, D_INNER), f32)
    b_w_out = jax.random.normal(ks[16], (nB, D_INNER, D_MODEL), f32) * D_INNER ** -0.5
    return {"x": x, "norm_w": norm_w, "final_norm_w": final_norm_w,
            "a_w_in": a_w_in, "a_ln_w": a_ln_w, "a_ln_b": a_ln_b, "a_w_s": a_w_s, "a_b_s": a_b_s,
            "a_w_out": a_w_out,
            "b_w_in": b_w_in, "b_conv_w": b_conv_w, "b_conv_b": b_conv_b, "b_dt_bias": b_dt_bias,
            "b_a_log": b_a_log, "b_d_skip": b_d_skip, "b_norm_w": b_norm_w, "b_w_out": b_w_out}


def reference(x, norm_w, final_norm_w, a_w_in, a_ln_w, a_ln_b, a_w_s, a_b_s, a_w_out,
              b_w_in, b_conv_w, b_conv_b, b_dt_bias, b_a_log, b_d_skip, b_norm_w, b_w_out):
    h = x
    for i in range(DEPTH):
        hn = rmsnorm(h, norm_w[i])
        k = i // N_MIXERS
        if i % N_MIXERS == 0:
            out = spatial_gating_mixer(hn, a_w_in[k], a_ln_w[k], a_ln_b[k], a_w_s[k], a_b_s[k], a_w_out[k])
        else:
            out = ssd_mixer(hn, b_w_in[k], b_conv_w[k], b_conv_b[k], b_dt_bias[k], b_a_log[k],
                            b_d_skip[k], b_norm_w[k], b_w_out[k])
        h = h + out
    return rmsnorm(h, final_norm_w)
```

```python
import numpy as np
import concourse.bass as bass
import concourse.mybir as mybir
from concourse.bass_utils import run_bass_kernel_spmd

F32 = mybir.dt.float32
BF16 = mybir.dt.bfloat16
AF = mybir.ActivationFunctionType
ALU = mybir.AluOpType

D_MODEL = 1024
SEQ = 4096
DEPTH = 4
D_INNER = 2048
B_IN_DIM = 6176
NORM_EPS = 1e-6
LN_EPS = 1e-5
T = 512
NCH = T // 128
RING = 3


class Sched:
    def __init__(self, nc):
        self.nc = nc
        self.ops = []
        self.res = {}

    def add(self, eng, fn, reads=(), writes=(), dma=None, ndma=1):
        idx = len(self.ops)
        deps = set()
        for k in reads:
            st = self.res.get(k)
            if st is not None and st[0] is not None:
                deps.add(st[0])
            if st is not None and isinstance(k, tuple) and k[0] == "ps":
                for r in st[1]:
                    if self.ops[r]["eng"] != eng:
                        deps.add(r)
        for k in writes:
            st = self.res.get(k)
            if st is not None:
                if st[0] is not None:
                    deps.add(st[0])
                deps.update(st[1])
        for k in reads:
            st = self.res.setdefault(k, [None, []])
            st[1].append(idx)
        for k in writes:
            self.res[k] = [idx, []]
        deps.discard(idx)
        self.ops.append(dict(eng=eng, fn=fn, deps=sorted(deps), dma=dma, signal=False, ndma=ndma))
        return idx

    def emit(self):
        nc = self.nc
        engs = {"pe": nc.tensor, "act": nc.scalar, "dve": nc.vector,
                "pool": nc.gpsimd, "sp": nc.sync}
        ops = self.ops

        def need(p, x):
            if p["dma"] is not None or x["dma"] is not None:
                return True
            if p["eng"] == x["eng"] == "pe":
                return False
            return True

        streams = set()
        for x in ops:
            if x["dma"] is not None:
                streams.add(x["dma"])
            for d in x["deps"]:
                p = ops[d]
                if need(p, x) and p["dma"] is None:
                    p["signal"] = True
        sems = {}
        for e in engs:
            sems["eng_" + e] = nc.alloc_semaphore(name="s_" + e)
        for s in sorted(streams):
            sems["dma_" + s] = nc.alloc_semaphore(name="d_" + s)
        cnt = {k: 0 for k in sems}
        waited = {}
        for x in ops:
            e = engs[x["eng"]]
            want = {}
            for d in x["deps"]:
                p = ops[d]
                if not need(p, x):
                    continue
                s, v = p["sig"]
                if want.get(s, 0) < v:
                    want[s] = v
            for s, v in want.items():
                if waited.get((x["eng"], s), 0) >= v:
                    continue
                e.wait_ge(sems[s], v)
                waited[(x["eng"], s)] = v
            ins = x["fn"](e)
            if x["dma"] is not None:
                s = "dma_" + x["dma"]
                lst = ins if isinstance(ins, (list, tuple)) else [ins]
                assert len(lst) == x["ndma"], (len(lst), x["ndma"])
                for i in lst:
                    cnt[s] += 16
                    i.then_inc(sems[s], 16)
                x["sig"] = (s, cnt[s])
            elif x["signal"]:
                s = "eng_" + x["eng"]
                cnt[s] += 1
                ins.then_inc(sems[s], 1)
                x["sig"] = (s, cnt[s])
            else:
                x["sig"] = None


class Builder:
    def __init__(self, layers, final, n_tiles):
        self.layers = list(layers)
        self.final = final
        self.n_tiles = n_tiles
        self.nc = bass.Bass("TRN2", target_bir_lowering=False)
        self.S = Sched(self.nc)
        self.bank_ptr = 0
        self.uid = 0
        self.hn_ready = None
        self.bank_pending = {}

    def dram_in(self, name, shape):
        return self.nc.dram_tensor(name, list(shape), F32, kind="ExternalInput").ap()

    def sb(self, name, shape, dt=F32):
        return self.nc.alloc_sbuf_tensor(name, list(shape), dt).ap()

    def banks(self, n):
        p = self.bank_ptr
        for _ in range(16):
            if p % n:
                p += n - p % n
            if p + n > 8:
                p = 0
            if not any(self.bank_pending.get(p + i, False) for i in range(n)):
                break
            p += n
        else:
            raise AssertionError("no free PSUM bank (all pending evacuation)")
        self.bank_ptr = (p + n) % 8
        for i in range(n):
            self.bank_pending[p + i] = True
        return p

    def bank_keys(self, b, n=1):
        return [("ps", b + i) for i in range(n)]

    def psf(self, b, n=1):
        return self.ps[:, b * 512:(b + n) * 512]

    def psb(self, b, n=1):
        return self.ps[:, b * 512:(b + n) * 512].bitcast(BF16)

    def op(self, eng, fn, reads=(), writes=(), dma=None, ndma=1):
        for kk in reads:
            if isinstance(kk, tuple) and kk[0] == "ps":
                self.bank_pending[kk[1]] = False
        if dma == "c":
            self.uid += 1
            dma = "c%d" % self.uid
        return self.S.add(eng, fn, reads, writes, dma, ndma)

    def rsqrt(self, out, x, scale, eps, rkeys, wkeys, tmp, tmpkey):
        self.op("act", lambda e: e.activation(out=tmp, in_=x, func=AF.Ln, scale=scale, bias=eps),
                reads=rkeys, writes=[tmpkey])
        self.op("act", lambda e: e.activation(out=out, in_=tmp, func=AF.Exp, scale=-0.5),
                reads=[tmpkey], writes=wkeys)

    def build(self):
        nc = self.nc
        NT = self.n_tiles
        L = NT * T
        self.x = self.dram_in("x", [L, D_MODEL])
        self.y = nc.dram_tensor("y", [L, D_MODEL], F32, kind="ExternalOutput").ap()
        self.a_w_in = self.dram_in("a_w_in", [2, D_MODEL, 3 * D_INNER])
        self.a_w_out = self.dram_in("a_w_out", [2, D_INNER, D_MODEL])
        self.b_w_in = self.dram_in("b_w_in", [2, D_MODEL, B_IN_DIM])
        self.b_w_out = self.dram_in("b_w_out", [2, D_INNER, D_MODEL])
        d_ident = self.dram_in("c_ident", [128, 128])
        d_tri = self.dram_in("c_tri", [128, 128])
        d_striu = self.dram_in("c_striu", [128, 128])
        d_normw = self.dram_in("p_normw_fm", [128, 4, 8])
        d_fnw = self.dram_in("p_fnw_bc", [128, D_MODEL])
        d_wsT = self.dram_in("p_a_wsT", [2, 128, 8, 128])
        d_lw = self.dram_in("p_a_lw_fm", [2, 128, 16])
        d_lb = self.dram_in("p_a_lb_fm", [2, 128, 16])
        d_bsb = self.dram_in("p_a_bs_bc", [2, 128, 8, 128])
        d_convw = self.dram_in("p_b_convw_fm", [2, 128, 32, 4])
        d_convb = self.dram_in("p_b_convb_fm", [2, 128, 32])
        d_alog = self.dram_in("p_b_alog_bc", [2, 128, 32])
        d_dtb = self.dram_in("p_b_dtb_bc", [2, 128, 32])
        d_dskip = self.dram_in("p_b_dskip_fm", [2, 128, 16])
        d_bnw = self.dram_in("p_b_nw_fm", [2, 128, 16])

        ps_t = nc.alloc_psum_tensor("ps", [128, 4096], F32)
        self.ps = ps_t.ap()

        sb = self.sb
        op = self.op
        self.ident_f = sb("ident_f", [128, 128])
        self.ident_b = sb("ident_b", [128, 128], BF16)
        self.tri_f = sb("tri_f", [128, 128])
        self.tri_b = sb("tri_b", [128, 128], BF16)
        self.striu_b = sb("striu_b", [128, 128], BF16)
        self.ones_b = sb("ones_b", [128, 128], BF16)
        self.normw = sb("normw", [128, 4, 8])
        self.d_fnw = d_fnw
        ctmp = sb("ctmp", [128, 128])

        def ld(dst, src, key, q="sp"):
            op(q, lambda e: e.dma_start(out=dst, in_=src), writes=[key], dma="c")

        ld(self.ident_f, d_ident, "ident_f")
        ld(self.tri_f, d_tri, "tri_f")
        ld(ctmp, d_striu, "ctmp")
        ld(self.normw, d_normw, "normw")
        op("dve", lambda e: e.tensor_copy(out=self.ident_b, in_=self.ident_f), reads=["ident_f"], writes=["ident_b"])
        op("dve", lambda e: e.tensor_copy(out=self.tri_b, in_=self.tri_f), reads=["tri_f"], writes=["tri_b"])
        op("dve", lambda e: e.tensor_copy(out=self.striu_b, in_=ctmp), reads=["ctmp"], writes=["striu_b"])
        op("dve", lambda e: e.memset(self.ones_b, 1.0), writes=["ones_b"])

        self.resid = sb("resid", [128, NCH, D_MODEL])
        self.hnT = sb("hnT", [128, 8, T], BF16)
        self.junk = sb("junk", [128, D_MODEL], BF16)
        self.yb = [sb("yb%d" % i, [128, 1024]) for i in range(2)]
        self.hs = self.yb[0]
        scr = sb("scr", [128, 3, 2048], BF16)
        self.xs = scr[:, 0, :].rearrange("p (h d) -> p h d", h=32)
        self.xdt = scr[:, 1, :].rearrange("p (h d) -> p h d", h=32)
        self.xw = scr[:, 2, :].rearrange("p (h d) -> p h d", h=32)
        self.zs_t = [scr[:, 0, i * 1024:(i + 1) * 1024].bitcast(F32) for i in range(2)]
        self.m2_t = [scr[:, 1, i * 1024:(i + 1) * 1024].bitcast(F32) for i in range(2)]
        self.g1_t = [scr[:, 2, i * 1024:(i + 1) * 1024].bitcast(F32) for i in range(2)]
        self.ss = sb("ss", [128, 8])
        self.rstd = sb("rstd", [128, 8])
        self.lntmp = sb("lntmp", [128, 8])
        self.ring = sb("ring", [128, RING, 8, 512], BF16)
        self.gT = sb("gT", [128, 16, T], BF16)
        self.big = sb("big", [128, NCH, D_INNER], BF16)

        self.wq = []
        self.wq_loaded = 0
        self.wq_used = 0

        kinds = ["A" if (li % 2 == 0) else "B" for li in self.layers]
        if "A" in kinds:
            self.setup_A(d_wsT, d_lw, d_lb, d_bsb)
        if "B" in kinds:
            self.setup_B(d_convw, d_convb, d_alog, d_dtb, d_dskip, d_bnw)

        for t in range(NT):
            for li in self.layers:
                if li % 2 == 0:
                    self.queue_A_weights(li // 2)
                else:
                    self.queue_B_weights(li // 2)

        self.blocks_per_tile = len(self.wq) // NT
        self.wscr = nc.dram_tensor("wscr", [self.blocks_per_tile, 128, 8, 512], BF16, kind="Internal").ap()
        xv = self.x.rearrange("(t c p) d -> t p c d", p=128, c=NCH)
        yv = self.y.rearrange("(t c p) d -> t p c d", p=128, c=NCH)
        out_keys = []
        self.xv, self.yv, self.out_keys = xv, yv, out_keys
        for c in range(NCH):
            self.load_x_chunk(0, c)
        for t in range(NT):
            self.cur_tile = t
            for idx, li in enumerate(self.layers):
                last = (idx + 1 == len(self.layers))
                self.next_li = None if last else self.layers[idx + 1]
                self.is_last = last
                self.do_final = last and self.final
                if li % 2 == 0:
                    self.layer_A(t, li)
                else:
                    self.layer_B(t, li)
            self.hn_ready = None
        op("sp", lambda e: None, reads=out_keys)
        self.S.ops[-1]["fn"] = lambda e: e.nop()
        self.S.emit()
        return nc

    def load_x_chunk(self, t, c):
        self.op("sp", lambda e, t=t, c=c: e.dma_start(out=self.resid[:, c, :], in_=self.xv[t][:, c, :]),
                writes=[("resid", c)], dma="x%d" % c)

    def store_y_chunk(self, t, c):
        self.op("sp", lambda e, t=t, c=c: e.dma_start(out=self.yv[t][:, c, :], in_=self.resid[:, c, :]),
                reads=[("resid", c)], writes=[("out", t, c)], dma="y%d" % c)
        self.out_keys.append(("out", t, c))
        if t + 1 < self.n_tiles:
            self.load_x_chunk(t + 1, c)

    def wslot(self, i):
        return self.ring[:, i % RING]

    def _emit_load(self, j):
        srcs = self.wq[j]
        slot = self.wslot(j)
        nb = self.blocks_per_tile
        tj, bj = j // nb, j % nb
        skey = ("ring", j % RING)
        stream = "w%d" % (j % RING)
        if tj == 0:
            def fn(e, srcs=srcs, slot=slot):
                res = []
                for (c0, c1, src) in srcs:
                    res.append(e.dma_start(out=slot[:, :, c0:c1], in_=src))
                return res
            self.op("pool", fn, writes=[skey], dma=stream, ndma=len(srcs))
            if self.n_tiles > 1:
                self.op("sp", lambda e, slot=slot, bj=bj: e.dma_start(out=self.wscr[bj], in_=slot),
                        reads=[skey], writes=[("wscr", bj)], dma="wb%d" % (j % RING))
        else:
            self.op("sp", lambda e, slot=slot, bj=bj: e.dma_start(out=slot, in_=self.wscr[bj]),
                    reads=[("wscr", bj)], writes=[skey], dma="h%d" % (j % RING))

    def use_block(self):
        i = self.wq_used
        self.wq_used += 1
        while self.wq_loaded < min(RING, len(self.wq)):
            self._emit_load(self.wq_loaded)
            self.wq_loaded += 1
        assert i < self.wq_loaded, (i, self.wq_loaded)
        return self.wslot(i), ("ring", i % RING), i

    def done_block(self, i):
        j = i + RING
        if j < len(self.wq):
            assert j == self.wq_loaded, (j, self.wq_loaded)
            self._emit_load(j)
            self.wq_loaded += 1

    def queue_A_weights(self, k):
        w_in = self.a_w_in[k].rearrange("(k p) n -> p k n", p=128)
        w_out = self.a_w_out[k]
        for j in range(4):
            c = 4096 + j * 512
            self.wq.append([(0, 512, w_in[:, :, c:c + 512])])
        for j in range(8):
            self.wq.append([(0, 256, w_in[:, :, j * 256:(j + 1) * 256]),
                            (256, 512, w_in[:, :, 2048 + j * 256:2048 + (j + 1) * 256])])
        self.queue_out_weights(w_out)

    def queue_out_weights(self, w_out):
        for ch in range(2):
            for kh in range(2):
                src = w_out[kh * 1024:(kh + 1) * 1024, ch * 512:(ch + 1) * 512].rearrange("(k p) n -> p k n", p=128)
                self.wq.append([(0, 512, src)])

    def queue_B_weights(self, k):
        w_in = self.b_w_in[k].rearrange("(k p) n -> p k n", p=128)
        for j in list(range(4, 12)) + list(range(4)):
            self.wq.append([(0, 512, w_in[:, :, j * 512:(j + 1) * 512])])
        self.queue_out_weights(self.b_w_out[k])

    def rmsnorm_to_hnT(self, li):
        if self.hn_ready == li:
            return
        for c in range(NCH):
            self.rmsnorm_chunk(li, c)

    def rmsnorm_chunk(self, li, c):
        self.rmsnorm_s1(li, c)
        self.rmsnorm_s2(li, c)

    def rmsnorm_s1(self, li, c):
        op = self.op
        if True:
            op("act", lambda e, c=c: e.activation(out=self.junk, in_=self.resid[:, c, :], func=AF.Square,
                                                  accum_out=self.ss[:, c:c + 1]),
               reads=[("resid", c)], writes=["junk", ("ss", c)])
            self.rsqrt(self.rstd[:, c:c + 1], self.ss[:, c:c + 1], 1.0 / D_MODEL, NORM_EPS,
                       [("ss", c)], [("rstd", c)], self.lntmp[:, c:c + 1], ("lntmp", c))
            op("act", lambda e, c=c: e.activation(out=self.hs, in_=self.resid[:, c, :], func=AF.Copy,
                                                  scale=self.rstd[:, c:c + 1]),
               reads=[("resid", c), ("rstd", c)], writes=["yb0"])

    def rmsnorm_s2(self, li, c):
        op = self.op
        if True:
            for half in range(2):
                b = self.banks(1)

                def tr(e, b=b, half=half):
                    for j in range(4):
                        k = half * 4 + j
                        i = e.transpose(out=self.psf(b)[:, j * 128:(j + 1) * 128],
                                        in_=self.hs[:, k * 128:(k + 1) * 128], identity=self.ident_f)
                    return i
                op("pe", tr, reads=["yb0", "ident_f"], writes=self.bank_keys(b))

                def ev(e, b=b, half=half, c=c):
                    return e.tensor_tensor(
                        out=self.hnT[:, half * 4:half * 4 + 4, c * 128:(c + 1) * 128],
                        in0=self.psf(b).rearrange("p (k t) -> p k t", k=4),
                        in1=self.normw[:, li, half * 4:half * 4 + 4].unsqueeze(2).to_broadcast([128, 4, 128]),
                        op=ALU.mult)
                op("dve", ev, reads=self.bank_keys(b) + ["normw"], writes=[("hnT", c)])

    def out_proj(self, next_li=None, do_final=False):
        op = self.op
        if do_final:
            self.final_norm_begin()
        for ch in range(2):
            s0, k0, i0 = self.use_block()
            s1, k1, i1 = self.use_block()
            for c in range(NCH):
                b = self.banks(1)

                def mm(e, b=b, c=c, s0=s0, s1=s1):
                    for kh, s in ((0, s0), (1, s1)):
                        for k in range(8):
                            i = e.matmul(self.psf(b), lhsT=self.gT[:, kh * 8 + k, c * 128:(c + 1) * 128],
                                         rhs=s[:, k, :], start=(kh == 0 and k == 0), stop=(kh == 1 and k == 7))
                    return i
                op("pe", mm, reads=[k0, k1, ("gT", c)], writes=self.bank_keys(b))

                def add(e, b=b, c=c, ch=ch):
                    dst = self.resid[:, c, ch * 512:(ch + 1) * 512]
                    return e.tensor_tensor(out=dst, in0=self.psf(b), in1=dst, op=ALU.add)
                op("dve", add, reads=self.bank_keys(b) + [("resid", c)], writes=[("resid", c)])
                if ch == 1:
                    if next_li is not None:
                        if c > 0:
                            self.rmsnorm_s2(next_li, c - 1)
                        self.rmsnorm_s1(next_li, c)
                    else:
                        if do_final:
                            self.final_norm_chunk(c)
                        if self.is_last:
                            self.store_y_chunk(self.cur_tile, c)
            if ch == 1 and next_li is not None:
                self.rmsnorm_s2(next_li, NCH - 1)
            self.done_block(i0)
            self.done_block(i1)
        if next_li is not None:
            self.hn_ready = next_li

    def final_norm_begin(self):
        self.op("sp", lambda e: e.dma_start(out=self.yb[1], in_=self.d_fnw), writes=["yb1"], dma="fnw")

    def final_norm_chunk(self, c):
        op = self.op
        fnw = self.yb[1]
        op("act", lambda e, c=c: e.activation(out=self.junk, in_=self.resid[:, c, :], func=AF.Square,
                                              accum_out=self.ss[:, c:c + 1]),
           reads=[("resid", c)], writes=["junk", ("ss", c)])
        self.rsqrt(self.rstd[:, c:c + 1], self.ss[:, c:c + 1], 1.0 / D_MODEL, NORM_EPS,
                   [("ss", c)], [("rstd", c)], self.lntmp[:, c:c + 1], ("lntmp", c))
        op("dve", lambda e, c=c: e.scalar_tensor_tensor(out=self.resid[:, c, :], in0=self.resid[:, c, :],
                                                        scalar=self.rstd[:, c:c + 1], in1=fnw,
                                                        op0=ALU.mult, op1=ALU.mult),
           reads=[("resid", c), ("rstd", c), "yb1"], writes=[("resid", c)])

    def setup_A(self, d_wsT, d_lw, d_lb, d_bsb):
        op = self.op
        sb = self.sb
        self.WcT = sb("WcT", [128, 2, 8, 128], BF16)
        self.lw = sb("lw", [128, 2, 16])
        self.lb = sb("lb", [128, 2, 16])
        self.bias2 = sb("bias2", [128, 2, 16, 128], BF16)
        self.st6 = sb("st6", [128, NCH, 4, 6])
        self.mv = sb("mv", [128, NCH, 2])
        self.vrstd = sb("vrstd", [128, NCH])
        self.vtmp = sb("vtmp", [128, NCH])
        bigf = self.big.rearrange("p c n -> p (c n)")
        wtmp = bigf[:, 0:2048].bitcast(F32).rearrange("p (g t) -> p g t", g=8)
        bsb = bigf[:, 2048:4096].bitcast(F32).rearrange("p (g t) -> p g t", g=8)
        bigk = [("big", c) for c in range(NCH)]
        for k in range(2):
            if (2 * k) not in self.layers:
                continue
            op("sp", lambda e, k=k: [e.dma_start(out=wtmp, in_=d_wsT[k]), e.dma_start(out=bsb, in_=d_bsb[k])],
               writes=bigk, dma="c", ndma=2)
            op("sp", lambda e, k=k: e.dma_start(out=self.lw[:, k, :], in_=d_lw[k]), writes=[("lw", k)], dma="c")
            op("sp", lambda e, k=k: e.dma_start(out=self.lb[:, k, :], in_=d_lb[k]), writes=[("lb", k)], dma="c")
            op("dve", lambda e, k=k: e.tensor_tensor(out=self.WcT[:, k], in0=wtmp,
                                                     in1=self.tri_f.unsqueeze(1).to_broadcast([128, 8, 128]),
                                                     op=ALU.mult),
               reads=bigk + ["tri_f"], writes=[("WcT", k)])
            b = self.banks(2)

            def rs(e, k=k, b=b):
                for g in range(8):
                    i = e.matmul(self.psf(b, 2)[:, g * 128:(g + 1) * 128], lhsT=self.ones_b,
                                 rhs=self.WcT[:, k, g, :], start=True, stop=True)
                return i
            op("pe", rs, reads=["ones_b", ("WcT", k)], writes=self.bank_keys(b, 2))
            for fb in range(16):
                g = fb // 2
                op("dve", lambda e, k=k, fb=fb, g=g, b=b: e.scalar_tensor_tensor(
                    out=self.bias2[:, k, fb, :], in0=self.psf(b, 2)[:, g * 128:(g + 1) * 128],
                    scalar=self.lb[:, k, fb:fb + 1], in1=bsb[:, g, :], op0=ALU.mult, op1=ALU.add),
                   reads=self.bank_keys(b, 2) + [("lb", k)] + bigk, writes=[("bias2", k)])

    def layer_A(self, t, li):
        op = self.op
        k = li // 2
        self.rmsnorm_to_hnT(li)
        vn = self.big
        for j in range(4):
            slot, skey, sidx = self.use_block()
            for c in range(NCH):
                b = self.banks(1)

                def mm(e, b=b, c=c, slot=slot):
                    for kk in range(8):
                        i = e.matmul(self.psf(b), lhsT=self.hnT[:, kk, c * 128:(c + 1) * 128],
                                     rhs=slot[:, kk, :], start=(kk == 0), stop=(kk == 7))
                    return i
                op("pe", mm, reads=[("hnT", c), skey], writes=self.bank_keys(b))
                op("act", lambda e, b=b, j=j, c=c: e.activation(out=vn[:, c, j * 512:(j + 1) * 512], in_=self.psf(b),
                                                                func=AF.Copy),
                   reads=self.bank_keys(b), writes=[("big", c), ("vcopy", b)])
                op("dve", lambda e, b=b, j=j, c=c: e.bn_stats(out=self.st6[:, c, j, :], in_=self.psf(b)),
                   reads=self.bank_keys(b) + [("vcopy", b)], writes=[("st6", c, j)])
            self.done_block(sidx)
        for c in range(NCH):
            op("dve", lambda e, c=c: e.bn_aggr(out=self.mv[:, c, :], in_=self.st6[:, c].rearrange("p j s -> p (j s)")),
               reads=[("st6", c, j) for j in range(4)], writes=[("mv", c)])
            self.rsqrt(self.vrstd[:, c:c + 1], self.mv[:, c, 1:2], 1.0, LN_EPS, [("mv", c)], [("vrstd", c)],
                       self.vtmp[:, c:c + 1], ("vtmp", c))
            op("dve", lambda e, c=c: e.tensor_scalar(
                out=vn[:, c, :], in0=vn[:, c, :], scalar1=self.mv[:, c, 0:1], scalar2=self.vrstd[:, c:c + 1],
                op0=ALU.subtract, op1=ALU.mult),
               reads=[("big", c), ("mv", c), ("vrstd", c)], writes=[("big", c)])
        for j in range(8):
            slot, skey, sidx = self.use_block()
            for fi in range(2):
                fb = 2 * j + fi
                g = fb // 2
                bz = self.banks(1)
                bu = self.banks(1)
                bm = self.banks(1)
                i2 = fb % 2

                def mmz(e, bz=bz, fi=fi, slot=slot):
                    for kk in range(8):
                        i = e.matmul(self.psf(bz), lhsT=slot[:, kk, fi * 128:(fi + 1) * 128],
                                     rhs=self.hnT[:, kk, :], start=(kk == 0), stop=(kk == 7))
                    return i
                op("pe", mmz, reads=[skey] + [("hnT", c) for c in range(NCH)], writes=self.bank_keys(bz))

                def mmu(e, bu=bu, fi=fi, slot=slot):
                    for kk in range(8):
                        i = e.matmul(self.psf(bu), lhsT=slot[:, kk, 256 + fi * 128:256 + (fi + 1) * 128],
                                     rhs=self.hnT[:, kk, :], start=(kk == 0), stop=(kk == 7))
                    return i
                op("pe", mmu, reads=[skey] + [("hnT", c) for c in range(NCH)], writes=self.bank_keys(bu))

                def mmm(e, bm=bm, fb=fb, g=g):
                    for c in range(NCH):
                        i = e.matmul(self.psf(bm)[:, c * 128:(c + 1) * 128],
                                     lhsT=vn[:, c, fb * 128:(fb + 1) * 128],
                                     rhs=self.WcT[:, k, g, :], start=True, stop=True)
                    return i
                op("pe", mmm, reads=[("big", c) for c in range(NCH)] + [("WcT", k)], writes=self.bank_keys(bm))

                zs = self.zs_t[i2]
                m2 = self.m2_t[i2]
                g1 = self.g1_t[i2]
                op("act", lambda e, bz=bz, zs=zs: e.activation(out=zs, in_=self.psf(bz), func=AF.Silu),
                   reads=self.bank_keys(bz), writes=[("X0", i2)])
                op("dve", lambda e, bm=bm, fb=fb, m2=m2: e.scalar_tensor_tensor(
                    out=m2.rearrange("p (c t) -> p c t", c=NCH),
                    in0=self.psf(bm).rearrange("p (c t) -> p c t", c=NCH),
                    scalar=self.lw[:, k, fb:fb + 1],
                    in1=self.bias2[:, k, fb, :].unsqueeze(1).to_broadcast([128, NCH, 128]),
                    op0=ALU.mult, op1=ALU.add),
                   reads=self.bank_keys(bm) + [("lw", k), ("bias2", k)], writes=[("X1", i2)])
                op("dve", lambda e, bu=bu, zs=zs, g1=g1: e.tensor_tensor(out=g1, in0=self.psf(bu), in1=zs, op=ALU.mult),
                   reads=self.bank_keys(bu) + [("X0", i2)], writes=[("X2", i2)])
                op("dve", lambda e, fb=fb, g1=g1, m2=m2: e.tensor_tensor(out=self.gT[:, fb, :], in0=g1, in1=m2, op=ALU.mult),
                   reads=[("X2", i2), ("X1", i2)], writes=[("gT", c) for c in range(NCH)])
            self.done_block(sidx)
        self.out_proj(self.next_li, self.do_final)

    def setup_B(self, d_convw, d_convb, d_alog, d_dtb, d_dskip, d_bnw):
        op = self.op
        sb = self.sb
        self.convw = sb("convw", [128, 2, 32, 4])
        self.convb = sb("convb", [128, 2, 32])
        self.Abc = sb("Abc", [128, 2, 32])
        self.dtb = sb("dtb", [128, 2, 32])
        self.dsk = sb("dsk", [128, 2, 16])
        self.bnw = sb("bnw", [128, 2, 16])
        self.Ddiag = sb("Ddiag", [128, 2, 16, 128], BF16)
        self.state = sb("state", [128, 2, D_INNER])
        self.carry = sb("carry", [128, 2, 32, 3], BF16)
        self.prev_b = sb("prev_b", [128, D_INNER], BF16)
        self.wdt = sb("wdt", [128, 8, 32], BF16)
        self.dtr = sb("dtr", [128, NCH, 32])
        self.dte1 = sb("dte1", [128, NCH, 32])
        self.dt = sb("dt", [128, NCH, 32])
        self.a = sb("a_dt", [128, NCH, 32])
        self.a_hi = sb("a_hi", [128, NCH, 32], BF16)
        self.a_lo = sb("a_lo", [128, NCH, 32], BF16)
        self.xraw = [sb("xraw%d" % i, [128, T + 4], BF16) for i in range(2)]
        self.dg = [sb("dg%d" % i, [128, 4, 128], BF16) for i in range(2)]
        self.xT = sb("xT", [128, 16, T], BF16)
        self.BT = sb("BT", [128, 8, T], BF16)
        self.CT = sb("CT", [128, 8, T], BF16)
        self.rb = [sb("rb%d" % i, [128, 4, 128], BF16) for i in range(3)]
        self.Eb = [sb("Eb%d" % i, [128, 4, 128], BF16) for i in range(3)]
        self.wp = [sb("wp%d" % i, [128, 4, 128], BF16) for i in range(3)]
        self.cbm = sb("cbm", [128, 8, 128], BF16)
        self.sc2 = sb("sc2", [128, 32])
        self.ecd = sb("ecd", [128, 96])
        self.Btok = sb("Btok", [128, 8, 128], BF16)
        self.gss = sb("gss", [128, 8])
        self.grstd = sb("grstd", [128, 8])
        self.gtmp = sb("gtmp", [128, 8])
        self.gn = sb("gn", [128, D_INNER], BF16)
        for k in range(2):
            if (2 * k + 1) not in self.layers:
                continue
            op("sp", lambda e, k=k: e.dma_start(out=self.convw[:, k], in_=d_convw[k]), writes=[("convw", k)], dma="c")
            op("sp", lambda e, k=k: e.dma_start(out=self.convb[:, k], in_=d_convb[k]), writes=[("convb", k)], dma="c")
            op("sp", lambda e, k=k: e.dma_start(out=self.Abc[:, k], in_=d_alog[k]), writes=[("Abc", k)], dma="c")
            op("sp", lambda e, k=k: e.dma_start(out=self.dtb[:, k], in_=d_dtb[k]), writes=[("dtb", k)], dma="c")
            op("sp", lambda e, k=k: e.dma_start(out=self.dsk[:, k], in_=d_dskip[k]), writes=[("dsk", k)], dma="c")
            op("sp", lambda e, k=k: e.dma_start(out=self.bnw[:, k], in_=d_bnw[k]), writes=[("bnw", k)], dma="c")
            op("act", lambda e, k=k: e.activation(out=self.Abc[:, k], in_=self.Abc[:, k], func=AF.Exp),
               reads=[("Abc", k)], writes=[("Abc", k)])
            op("dve", lambda e, k=k: e.tensor_scalar(out=self.Abc[:, k], in0=self.Abc[:, k], scalar1=-1.0,
                                                     scalar2=None, op0=ALU.mult),
               reads=[("Abc", k)], writes=[("Abc", k)])
            for fb in range(16):
                op("dve", lambda e, k=k, fb=fb: e.tensor_scalar(out=self.Ddiag[:, k, fb, :], in0=self.ident_f,
                                                               scalar1=self.dsk[:, k, fb:fb + 1], scalar2=None,
                                                               op0=ALU.mult),
                   reads=["ident_f", ("dsk", k)], writes=[("Ddiag", k)])
            op("dve", lambda e, k=k: e.memset(self.state[:, k], 0.0), writes=[("state", k, 0), ("state", k, 1)])
            op("dve", lambda e, k=k: e.memset(self.carry[:, k], 0.0), writes=[("carry", k)])

    def layer_B(self, t, li):
        op = self.op
        k = li // 2
        allc = list(range(NCH))
        self.rmsnorm_to_hnT(li)
        op("act", lambda e: e.activation(out=self.prev_b, in_=self.state[:, k], func=AF.Copy),
           reads=[("state", k, 0), ("state", k, 1)], writes=[("prev_b", 0), ("prev_b", 1)])
        wdt_src = self.b_w_in[k].rearrange("(k p) n -> p k n", p=128)[:, :, 6144:6176]
        op("pool", lambda e: e.dma_start(out=self.wdt, in_=wdt_src), writes=["wdt"], dma="wdt")
        b = self.banks(1)

        def mmdt(e, b=b):
            for c in range(NCH):
                for kk in range(8):
                    i = e.matmul(self.psf(b)[:, c * 32:(c + 1) * 32], lhsT=self.hnT[:, kk, c * 128:(c + 1) * 128],
                                 rhs=self.wdt[:, kk, :], start=(kk == 0), stop=(kk == 7))
            return i
        op("pe", mmdt, reads=["wdt"] + [("hnT", c) for c in allc], writes=self.bank_keys(b))
        op("dve", lambda e, b=b: e.tensor_tensor(out=self.dtr, in0=self.psf(b)[:, 0:NCH * 32].rearrange("p (c h) -> p c h", c=NCH),
                                                 in1=self.dtb[:, k, :].unsqueeze(1).to_broadcast([128, NCH, 32]), op=ALU.add),
           reads=self.bank_keys(b) + [("dtb", k)], writes=["dtr"])
        op("act", lambda e: e.activation(out=self.dte1, in_=self.dtr, func=AF.Exp), reads=["dtr"], writes=["dte1"])
        op("act", lambda e: e.activation(out=self.dt, in_=self.dte1, func=AF.Ln, bias=1.0, scale=1.0),
           reads=["dte1"], writes=["dt"])
        op("dve", lambda e: e.tensor_tensor(out=self.a, in0=self.dt,
                                            in1=self.Abc[:, k, :].unsqueeze(1).to_broadcast([128, NCH, 32]), op=ALU.mult),
           reads=["dt", ("Abc", k)], writes=["a"])
        op("dve", lambda e: e.tensor_copy(out=self.a_hi, in_=self.a), reads=["a"], writes=["a_hi"])
        op("dve", lambda e: e.tensor_tensor(out=self.a_lo, in0=self.a, in1=self.a_hi, op=ALU.subtract),
           reads=["a", "a_hi"], writes=["a_lo"])
        def z_unit(j):
            slot, skey, sidx = self.use_block()
            for c in range(NCH):
                b = self.banks(1)

                def mm(e, b=b, c=c, slot=slot):
                    for kk in range(8):
                        i = e.matmul(self.psf(b), lhsT=self.hnT[:, kk, c * 128:(c + 1) * 128],
                                     rhs=slot[:, kk, :], start=(kk == 0), stop=(kk == 7))
                    return i
                op("pe", mm, reads=[skey, ("hnT", c)], writes=self.bank_keys(b))
                op("act", lambda e, b=b, c=c, j=j: e.activation(out=self.big[:, c, j * 512:(j + 1) * 512],
                                                                in_=self.psf(b), func=AF.Silu),
                   reads=self.bank_keys(b), writes=[("big", c, j // 2)] + ([("big", c)] if j % 2 == 0 else []))
            self.done_block(sidx)

        pending = []

        def conv_part(cb, b1):
            i2 = cb % 2
            xr = self.xraw[i2]
            dg = self.dg[i2]
            op("act", lambda e, xr=xr, cb=cb: e.activation(out=xr[:, 0:3], in_=self.carry[:, k, cb, :], func=AF.Copy),
               reads=[("carry", k)], writes=[("xraw", i2)])
            op("act", lambda e, xr=xr, b1=b1: e.activation(out=xr[:, 3:3 + T], in_=self.psf(b1), func=AF.Copy),
               reads=self.bank_keys(b1) + [("xraw", i2)], writes=[("xraw", i2)])
            op("act", lambda e, xr=xr, cb=cb: e.activation(out=self.carry[:, k, cb, :], in_=xr[:, T:T + 3], func=AF.Copy),
               reads=[("xraw", i2)], writes=[("carry", k)])
            for tap in range(4):
                op("dve", lambda e, dg=dg, tap=tap, cb=cb: e.tensor_scalar(
                    out=dg[:, tap, :], in0=self.ident_b, scalar1=self.convw[:, k, cb, tap:tap + 1],
                    scalar2=None, op0=ALU.mult),
                   reads=["ident_b", ("convw", k)], writes=[("dg", i2)])
            b2 = self.banks(1)

            def mmc(e, b2=b2, xr=xr, dg=dg):
                for tap in range(4):
                    i = e.matmul(self.psf(b2), lhsT=dg[:, tap, :], rhs=xr[:, tap:tap + T],
                                 start=(tap == 0), stop=(tap == 3))
                return i
            op("pe", mmc, reads=[("xraw", i2), ("dg", i2)], writes=self.bank_keys(b2))
            if cb < 16:
                dst, dkey = self.xT[:, cb, :], "xT"
            elif cb < 24:
                dst, dkey = self.BT[:, cb - 16, :], "BT"
            else:
                dst, dkey = self.CT[:, cb - 24, :], "CT"
            op("act", lambda e, b2=b2, dst=dst, cb=cb: e.activation(out=dst, in_=self.psf(b2), func=AF.Silu,
                                                                    bias=self.convb[:, k, cb:cb + 1]),
               reads=self.bank_keys(b2) + [("convb", k)], writes=[dkey])

        for j in range(8):
            slot, skey, sidx = self.use_block()
            for q in range(4):
                cb = 4 * j + q
                b1 = self.banks(1)

                def mm(e, b1=b1, q=q, slot=slot):
                    for kk in range(8):
                        i = e.matmul(self.psf(b1), lhsT=slot[:, kk, q * 128:(q + 1) * 128],
                                     rhs=self.hnT[:, kk, :], start=(kk == 0), stop=(kk == 7))
                    return i
                op("pe", mm, reads=[skey] + [("hnT", c) for c in allc], writes=self.bank_keys(b1))
                if pending:
                    conv_part(*pending.pop())
                pending.append((cb, b1))
            self.done_block(sidx)
        conv_part(*pending.pop())
        def head(c):
            cc = slice(c * 128, (c + 1) * 128)
            bcb = self.banks(2)

            def mmcb(e, bcb=bcb, cc=cc):
                for g in range(8):
                    i = e.matmul(self.psf(bcb, 2)[:, g * 128:(g + 1) * 128], lhsT=self.BT[:, g, cc],
                                 rhs=self.CT[:, g, cc], start=True, stop=True)
                return i
            op("pe", mmcb, reads=["BT", "CT"], writes=self.bank_keys(bcb, 2))
            op("dve", lambda e, bcb=bcb: e.tensor_tensor(out=self.cbm, in0=self.psf(bcb, 2).rearrange("p (g l) -> p g l", g=8),
                                                         in1=self.tri_b.unsqueeze(1).to_broadcast([128, 8, 128]), op=ALU.mult),
               reads=self.bank_keys(bcb, 2) + ["tri_b"], writes=["cbm"])
            bx = self.banks(2)

            def trx(e, bx=bx, cc=cc):
                for fb in range(16):
                    i = e.transpose(out=self.psb(bx, 2)[:, fb * 128:(fb + 1) * 128], in_=self.xT[:, fb, cc],
                                    identity=self.ident_b)
                return i
            op("pe", trx, reads=["xT", "ident_b"], writes=self.bank_keys(bx, 2))
            bB = self.banks(1)

            def trb(e, bB=bB, cc=cc):
                for g in range(8):
                    i = e.transpose(out=self.psb(bB)[:, g * 128:(g + 1) * 128], in_=self.BT[:, g, cc],
                                    identity=self.ident_b)
                return i
            op("pe", trb, reads=["BT", "ident_b"], writes=self.bank_keys(bB))
            op("act", lambda e, bB=bB: e.activation(out=self.Btok.rearrange("p g n -> p (g n)"), in_=self.psb(bB), func=AF.Copy),
               reads=self.bank_keys(bB), writes=["Btok"])
            bcs = self.banks(1)

            def mmcs(e, bcs=bcs, c=c):
                first = True
                for (lo, lt) in ((0, self.tri_b), (32, self.ones_b), (64, self.striu_b)):
                    for av in (self.a_hi, self.a_lo):
                        i = e.matmul(self.psf(bcs)[:, lo:lo + 32], lhsT=lt, rhs=av[:, c, :], start=first,
                                     stop=(lo == 64 and av is self.a_lo))
                        first = False
                return i
            op("pe", mmcs, reads=["tri_b", "ones_b", "striu_b", "a_hi", "a_lo"], writes=self.bank_keys(bcs))
            op("act", lambda e, bcs=bcs: e.activation(out=self.ecd, in_=self.psf(bcs)[:, 0:96], func=AF.Exp),
               reads=self.bank_keys(bcs), writes=["ecd"])
            op("dve", lambda e, c=c: e.tensor_tensor(out=self.sc2, in0=self.ecd[:, 64:96], in1=self.dt[:, c, :], op=ALU.mult),
               reads=["ecd", "dt"], writes=["sc2"])
            for hf in range(2):
                hh = slice(16 * hf, 16 * hf + 16)
                op("dve", lambda e, c=c, hh=hh, hf=hf, bx=bx: e.tensor_tensor(
                    out=self.xdt[:, hh, :],
                    in0=self.psb(bx, 2)[:, hf * 1024:(hf + 1) * 1024].rearrange("p (h d) -> p h d", h=16),
                    in1=self.dt[:, c, hh].unsqueeze(2).to_broadcast([128, 16, 64]), op=ALU.mult),
                   reads=self.bank_keys(bx, 2) + ["dt"], writes=[("X1", hf)])
            for hf in range(2):
                hh = slice(16 * hf, 16 * hf + 16)
                op("dve", lambda e, hh=hh, hf=hf, bx=bx: e.tensor_tensor(
                    out=self.xw[:, hh, :],
                    in0=self.psb(bx, 2)[:, hf * 1024:(hf + 1) * 1024].rearrange("p (h d) -> p h d", h=16),
                    in1=self.sc2[:, hh].unsqueeze(2).to_broadcast([128, 16, 64]), op=ALU.mult),
                   reads=self.bank_keys(bx, 2) + ["sc2"], writes=[("X2", hf)])

        def groups(c, prev=None):
            cc = slice(c * 128, (c + 1) * 128)
            half_banks = {}

            def stageA(g, c=c):
                i3 = g % 3
                rb, Eb = self.rb[i3], self.Eb[i3]
                for jh in range(4):
                    h = 4 * g + jh
                    if jh < 3:
                        op("dve", lambda e, rb=rb, jh=jh, h=h: e.tensor_scalar(
                            out=rb[:, jh, :], in0=self.tri_b, scalar1=self.a[:, c, h:h + 1], scalar2=None, op0=ALU.mult),
                           reads=["tri_b", "a"], writes=[("rbd", i3)])
                    else:
                        op("act", lambda e, rb=rb, jh=jh, h=h: e.activation(
                            out=rb[:, jh, :], in_=self.tri_b, func=AF.Copy, scale=self.a[:, c, h:h + 1]),
                           reads=["tri_b", "a"], writes=[("rba", i3)])

            def stageA2(g, c=c):
                i3 = g % 3
                rb, Eb = self.rb[i3], self.Eb[i3]
                bD = self.banks(1)
                op("pe", lambda e, bD=bD, rb=rb: e.matmul(self.psf(bD), lhsT=self.striu_b,
                                                          rhs=rb.rearrange("p j l -> p (j l)"), start=True, stop=True),
                   reads=["striu_b", ("rbd", i3), ("rba", i3)], writes=self.bank_keys(bD))
                op("act", lambda e, bD=bD, Eb=Eb: e.activation(out=Eb.rearrange("p j l -> p (j l)"), in_=self.psf(bD), func=AF.Exp),
                   reads=self.bank_keys(bD), writes=[("Eb", i3)])

            def stageB(g, c=c, cc=cc):
                i3 = g % 3
                Eb, wp = self.Eb[i3], self.wp[i3]
                pass

            def stageB1(g, c=c):
                i3 = g % 3
                Eb, wp = self.Eb[i3], self.wp[i3]
                op("dve", lambda e, Eb=Eb, wp=wp, g=g: e.tensor_tensor(
                    out=wp, in0=Eb, in1=self.cbm[:, g, :].unsqueeze(1).to_broadcast([128, 4, 128]), op=ALU.mult),
                   reads=[("Eb", i3), "cbm"], writes=[("wp", i3)])

            def stageB2(g, c=c, cc=cc):
                i3 = g % 3
                wp = self.wp[i3]
                hf, gl = g // 4, g % 4
                if gl == 0:
                    half_banks[hf] = (self.banks(2), self.banks(2))
                bA, bG = half_banks[hf]
                yb = self.yb[hf]

                def mmy(e, bA=bA, gl=gl, g=g, wp=wp):
                    for fi in range(2):
                        fb = 2 * g + fi
                        col = (2 * gl + fi) * 128
                        i = e.matmul(self.psf(bA, 2)[:, col:col + 128], lhsT=self.xT[:, fb, cc],
                                     rhs=self.Ddiag[:, k, fb, :], start=(gl % 2 == 0 and fi == 0), stop=False)
                    for jh in range(4):
                        h = 4 * g + jh
                        col = (4 * gl + jh) * 64
                        i = e.matmul(self.psf(bA, 2)[:, col:col + 64], lhsT=wp[:, jh, :],
                                     rhs=self.xdt[:, h, :], start=False, stop=False)
                    return i
                op("pe", mmy, reads=["xT", ("Ddiag", k), ("wp", i3), ("X1", hf)], writes=self.bank_keys(bA, 2))
                op("pe", lambda e, bG=bG, gl=gl, g=g: e.matmul(
                    self.psf(bG, 2)[:, gl * 256:(gl + 1) * 256], lhsT=self.CT[:, g, cc],
                    rhs=self.prev_b[:, g * 256:(g + 1) * 256], start=True, stop=True),
                   reads=["CT", ("prev_b", hf)], writes=self.bank_keys(bG, 2))
                if gl != 3:
                    return
                tb = self.xs[:, 16 * hf:16 * hf + 16, :]
                op("dve", lambda e, bG=bG, tb=tb, hf=hf: e.tensor_tensor(
                    out=tb, in0=self.psf(bG, 2).rearrange("p (h d) -> p h d", h=16),
                    in1=self.ecd[:, 16 * hf:16 * hf + 16].unsqueeze(2).to_broadcast([128, 16, 64]), op=ALU.mult),
                   reads=self.bank_keys(bG, 2) + ["ecd"], writes=[("X0", hf)])

                def mmadd(e, bA=bA, tb=tb):
                    tf = tb.rearrange("p h d -> p (h d)")
                    for i in range(2):
                        ins = e.matmul(self.psf(bA + i), lhsT=self.ident_b, rhs=tf[:, i * 512:(i + 1) * 512],
                                       start=False, stop=True)
                    return ins
                op("pe", mmadd, reads=["ident_b", ("X0", hf)], writes=self.bank_keys(bA, 2))
                op("dve", lambda e, bA=bA, yb=yb, hf=hf: e.tensor_tensor(
                    out=yb, in0=self.psf(bA, 2), in1=self.big[:, c, hf * 1024:(hf + 1) * 1024], op=ALU.mult),
                   reads=self.bank_keys(bA, 2) + [("big", c, hf)], writes=["yb%d" % hf])
                for gl2 in range(4):
                    g2 = 4 * hf + gl2
                    op("act", lambda e, yb=yb, gl2=gl2, g2=g2: e.activation(out=self.junk[:, 0:256], in_=yb[:, gl2 * 256:(gl2 + 1) * 256],
                                                                          func=AF.Square, accum_out=self.gss[:, g2:g2 + 1]),
                       reads=["yb%d" % hf], writes=["junk", ("gss", hf)])

            for step in range(-3, 8):
                if 0 <= step + 3 < 8:
                    stageA(step + 3)
                if 0 <= step + 2 < 8:
                    stageA2(step + 2)
                if 0 <= step + 1 < 8:
                    stageB1(step + 1)
                if 0 <= step < 8:
                    stageB2(step)
                if prev is not None:
                    if step == 1:
                        tailG1(prev)
                    elif step == 3:
                        tailG2(prev)
                    elif step == 4:
                        tailG3(prev)
                elif c == 0:
                    if step == -2:
                        z_unit(1)
                    elif step == 1:
                        z_unit(2)
                    elif step == 4:
                        z_unit(3)

        def tailG1(c):
            self.rsqrt(self.grstd, self.gss, 1.0 / 256.0, NORM_EPS, [("gss", 0), ("gss", 1)], ["grstd"], self.gtmp, "gtmp")
            for g in range(8):
                hf, gl = g // 4, g % 4
                op("act", lambda e, g=g, hf=hf, gl=gl: e.activation(out=self.gn[:, g * 256:(g + 1) * 256],
                                                                    in_=self.yb[hf][:, gl * 256:(gl + 1) * 256],
                                                                    func=AF.Copy, scale=self.grstd[:, g:g + 1]),
                   reads=["yb%d" % hf, "grstd"], writes=["gn"])

        tg_bank = {}

        def tailG2(c):
            bT = self.banks(2)
            tg_bank[c] = bT

            def trg(e, bT=bT):
                for fb in range(16):
                    i = e.transpose(out=self.psb(bT, 2)[:, fb * 128:(fb + 1) * 128], in_=self.gn[:, fb * 128:(fb + 1) * 128],
                                    identity=self.ident_b)
                return i
            op("pe", trg, reads=["gn", "ident_b"], writes=self.bank_keys(bT, 2))

        def tailG3(c):
            cc = slice(c * 128, (c + 1) * 128)
            bT = tg_bank[c]
            op("dve", lambda e, bT=bT, cc=cc: e.tensor_tensor(
                out=self.gT[:, :, cc], in0=self.psb(bT, 2).rearrange("p (f t) -> p f t", f=16),
                in1=self.bnw[:, k, :].unsqueeze(2).to_broadcast([128, 16, 128]), op=ALU.mult),
               reads=self.bank_keys(bT, 2) + [("bnw", k)], writes=[("gT", c)])

        def tailS(c):
            cc = slice(c * 128, (c + 1) * 128)
            bS = self.banks(4)

            def mms(e, bS=bS):
                for g in range(8):
                    i = e.matmul(self.psf(bS, 4)[:, g * 256:(g + 1) * 256], lhsT=self.Btok[:, g, :],
                                 rhs=self.xw[:, 4 * g:4 * g + 4, :].rearrange("p h d -> p (h d)"), start=True, stop=True)
                return i
            op("pe", mms, reads=["Btok", ("X2", 0), ("X2", 1)], writes=self.bank_keys(bS, 4))
            for hf in range(2):
                hs_ = slice(hf * 1024, (hf + 1) * 1024)
                op("pool", lambda e, hf=hf, hs_=hs_: e.tensor_tensor(
                    out=self.state[:, k, hs_].rearrange("p (h d) -> p h d", h=16),
                    in0=self.state[:, k, hs_].rearrange("p (h d) -> p h d", h=16),
                    in1=self.ecd[:, 32 + 16 * hf:48 + 16 * hf].unsqueeze(2).to_broadcast([128, 16, 64]), op=ALU.mult),
                   reads=[("state", k, hf), "ecd", ("prev_b", hf)], writes=[("state", k, hf)])
                op("dve", lambda e, bS=bS, hs_=hs_: e.tensor_tensor(out=self.state[:, k, hs_], in0=self.psf(bS, 4)[:, hs_],
                                                                  in1=self.state[:, k, hs_], op=ALU.add),
                   reads=self.bank_keys(bS, 4) + [("state", k, hf)], writes=[("state", k, hf)])
                op("act", lambda e, hs_=hs_: e.activation(out=self.prev_b[:, hs_], in_=self.state[:, k, hs_], func=AF.Copy),
                   reads=[("state", k, hf)], writes=[("prev_b", hf)])

        head(0)
        z_unit(0)
        for c in range(NCH):
            groups(c, prev=(c - 1 if c > 0 else None))
            tailS(c)
            if c + 1 < NCH:
                head(c + 1)
        tailG1(NCH - 1)
        tailG2(NCH - 1)
        tailG3(NCH - 1)
        self.out_proj(self.next_li, self.do_final)


def _host_params(p):
    f = np.float32
    out = {}
    k = np.arange(128)
    out["c_ident"] = np.eye(128, dtype=f)
    out["c_tri"] = (k[:, None] <= k[None, :]).astype(f)
    out["c_striu"] = (k[:, None] > k[None, :]).astype(f)
    out["p_normw_fm"] = np.ascontiguousarray(p["norm_w"].reshape(4, 8, 128).transpose(2, 0, 1)).astype(f)
    out["p_fnw_bc"] = np.ascontiguousarray(np.broadcast_to(p["final_norm_w"][None, :], (128, D_MODEL))).astype(f)
    out["p_a_wsT"] = np.ascontiguousarray(p["a_w_s"].transpose(0, 3, 1, 2)).astype(f)
    out["p_a_lw_fm"] = np.ascontiguousarray(p["a_ln_w"].reshape(2, 16, 128).transpose(0, 2, 1)).astype(f)
    out["p_a_lb_fm"] = np.ascontiguousarray(p["a_ln_b"].reshape(2, 16, 128).transpose(0, 2, 1)).astype(f)
    out["p_a_bs_bc"] = np.ascontiguousarray(np.broadcast_to(p["a_b_s"][:, None, :, :], (2, 128, 8, 128))).astype(f)
    out["p_b_convw_fm"] = np.ascontiguousarray(p["b_conv_w"].reshape(2, 4, 32, 128).transpose(0, 3, 2, 1)).astype(f)
    out["p_b_convb_fm"] = np.ascontiguousarray(p["b_conv_b"].reshape(2, 32, 128).transpose(0, 2, 1)).astype(f)
    out["p_b_alog_bc"] = np.ascontiguousarray(np.broadcast_to(p["b_a_log"][:, None, :], (2, 128, 32))).astype(f)
    out["p_b_dtb_bc"] = np.ascontiguousarray(np.broadcast_to(p["b_dt_bias"][:, None, :], (2, 128, 32))).astype(f)
    dsk = np.repeat(p["b_d_skip"], 64, axis=1)
    out["p_b_dskip_fm"] = np.ascontiguousarray(dsk.reshape(2, 16, 128).transpose(0, 2, 1)).astype(f)
    out["p_b_nw_fm"] = np.ascontiguousarray(p["b_norm_w"].reshape(2, 16, 128).transpose(0, 2, 1)).astype(f)
    for n in ("a_w_in", "a_w_out", "b_w_in", "b_w_out"):
        out[n] = np.ascontiguousarray(p[n]).astype(f)
    return out


_NC_CACHE = {}


def run_layers(x, params, layers, final, n_cores=None):
    B, L, _ = x.shape
    n_tiles = L // T
    key = (tuple(layers), bool(final), n_tiles)
    if key not in _NC_CACHE:
        _NC_CACHE[key] = Builder(layers, final, n_tiles).build()
    nc = _NC_CACHE[key]
    hp = _host_params(params)
    in_maps = []
    for b in range(B):
        m = dict(hp)
        m["x"] = np.ascontiguousarray(x[b]).astype(np.float32)
        in_maps.append(m)
    res = run_bass_kernel_spmd(nc, in_maps, core_ids=list(range(B)))
    return np.stack([r["y"] for r in res.results], axis=0)


def kernel(**inputs):
    x = np.asarray(inputs["x"], dtype=np.float32)
    params = {k: np.asarray(v, dtype=np.float32) for k, v in inputs.items() if k != "x"}
    return run_layers(x, params, list(range(DEPTH)), True)
```

```python
import numpy as np
import concourse.bass as bass
import concourse.mybir as mybir
from concourse.bass_utils import run_bass_kernel_spmd

F32 = mybir.dt.float32
BF16 = mybir.dt.bfloat16
AF = mybir.ActivationFunctionType
ALU = mybir.AluOpType

D_MODEL = 1024
SEQ = 4096
DEPTH = 4
D_INNER = 2048
B_IN_DIM = 6176
NORM_EPS = 1e-6
LN_EPS = 1e-5
T = 512
NCH = T // 128
RING = 3


class Sched:
    def __init__(self, nc):
        self.nc = nc
        self.ops = []
        self.res = {}

    def add(self, eng, fn, reads=(), writes=(), dma=None, ndma=1):
        idx = len(self.ops)
        deps = set()
        for k in reads:
            st = self.res.get(k)
            if st is not None and st[0] is not None:
                deps.add(st[0])
            if st is not None and isinstance(k, tuple) and k[0] == "ps":
                for r in st[1]:
                    if self.ops[r]["eng"] != eng:
                        deps.add(r)
        for k in writes:
            st = self.res.get(k)
            if st is not None:
                if st[0] is not None:
                    deps.add(st[0])
                deps.update(st[1])
        for k in reads:
            st = self.res.setdefault(k, [None, []])
            st[1].append(idx)
        for k in writes:
            self.res[k] = [idx, []]
        deps.discard(idx)
        self.ops.append(dict(eng=eng, fn=fn, deps=sorted(deps), dma=dma, signal=False, ndma=ndma))
        return idx

    def emit(self):
        nc = self.nc
        engs = {"pe": nc.tensor, "act": nc.scalar, "dve": nc.vector,
                "pool": nc.gpsimd, "sp": nc.sync}
        ops = self.ops

        def need(p, x):
            if p["dma"] is not None or x["dma"] is not None:
                return True
            if p["eng"] == x["eng"] == "pe":
                return False
            return True

        streams = set()
        for x in ops:
            if x["dma"] is not None:
                streams.add(x["dma"])
            for d in x["deps"]:
                p = ops[d]
                if need(p, x) and p["dma"] is None:
                    p["signal"] = True
        sems = {}
        for e in engs:
            sems["eng_" + e] = nc.alloc_semaphore(name="s_" + e)
        for s in sorted(streams):
            sems["dma_" + s] = nc.alloc_semaphore(name="d_" + s)
        cnt = {k: 0 for k in sems}
        waited = {}
        for x in ops:
            e = engs[x["eng"]]
            want = {}
            for d in x["deps"]:
                p = ops[d]
                if not need(p, x):
                    continue
                s, v = p["sig"]
                if want.get(s, 0) < v:
                    want[s] = v
            for s, v in want.items():
                if waited.get((x["eng"], s), 0) >= v:
                    continue
                e.wait_ge(sems[s], v)
                waited[(x["eng"], s)] = v
            ins = x["fn"](e)
            if x["dma"] is not None:
                s = "dma_" + x["dma"]
                lst = ins if isinstance(ins, (list, tuple)) else [ins]
                assert len(lst) == x["ndma"], (len(lst), x["ndma"])
                for i in lst:
                    cnt[s] += 16
                    i.then_inc(sems[s], 16)
                x["sig"] = (s, cnt[s])
            elif x["signal"]:
                s = "eng_" + x["eng"]
                cnt[s] += 1
                ins.then_inc(sems[s], 1)
                x["sig"] = (s, cnt[s])
            else:
                x["sig"] = None


class Builder:
    def __init__(self, layers, final, n_tiles):
        self.layers = list(layers)
        self.final = final
        self.n_tiles = n_tiles
        self.nc = bass.Bass("TRN2", target_bir_lowering=False)
        self.S = Sched(self.nc)
        self.bank_ptr = 0
        self.uid = 0
        self.hn_ready = None
        self.bank_pending = {}

    def dram_in(self, name, shape):
        return self.nc.dram_tensor(name, list(shape), F32, kind="ExternalInput").ap()

    def sb(self, name, shape, dt=F32):
        return self.nc.alloc_sbuf_tensor(name, list(shape), dt).ap()

    def banks(self, n):
        p = self.bank_ptr
        for _ in range(16):
            if p % n:
                p += n - p % n
            if p + n > 8:
                p = 0
            if not any(self.bank_pending.get(p + i, False) for i in range(n)):
                break
            p += n
        else:
            raise AssertionError("no free PSUM bank (all pending evacuation)")
        self.bank_ptr = (p + n) % 8
        for i in range(n):
            self.bank_pending[p + i] = True
        return p

    def bank_keys(self, b, n=1):
        return [("ps", b + i) for i in range(n)]

    def psf(self, b, n=1):
        return self.ps[:, b * 512:(b + n) * 512]

    def psb(self, b, n=1):
        return self.ps[:, b * 512:(b + n) * 512].bitcast(BF16)

    def op(self, eng, fn, reads=(), writes=(), dma=None, ndma=1):
        for kk in reads:
            if isinstance(kk, tuple) and kk[0] == "ps":
                self.bank_pending[kk[1]] = False
        if dma == "c":
            self.uid += 1
            dma = "c%d" % self.uid
        return self.S.add(eng, fn, reads, writes, dma, ndma)

    def rsqrt(self, out, x, scale, eps, rkeys, wkeys, tmp, tmpkey):
        self.op("act", lambda e: e.activation(out=tmp, in_=x, func=AF.Ln, scale=scale, bias=eps),
                reads=rkeys, writes=[tmpkey])
        self.op("act", lambda e: e.activation(out=out, in_=tmp, func=AF.Exp, scale=-0.5),
                reads=[tmpkey], writes=wkeys)

    def build(self):
        nc = self.nc
        NT = self.n_tiles
        L = NT * T
        self.x = self.dram_in("x", [L, D_MODEL])
        self.y = nc.dram_tensor("y", [L, D_MODEL], F32, kind="ExternalOutput").ap()
        self.a_w_in = self.dram_in("a_w_in", [2, D_MODEL, 3 * D_INNER])
        self.a_w_out = self.dram_in("a_w_out", [2, D_INNER, D_MODEL])
        self.b_w_in = self.dram_in("b_w_in", [2, D_MODEL, B_IN_DIM])
        self.b_w_out = self.dram_in("b_w_out", [2, D_INNER, D_MODEL])
        d_ident = self.dram_in("c_ident", [128, 128])
        d_tri = self.dram_in("c_tri", [128, 128])
        d_striu = self.dram_in("c_striu", [128, 128])
        d_normw = self.dram_in("p_normw_fm", [128, 4, 8])
        d_fnw = self.dram_in("p_fnw_bc", [128, D_MODEL])
        d_wsT = self.dram_in("p_a_wsT", [2, 128, 8, 128])
        d_lw = self.dram_in("p_a_lw_fm", [2, 128, 16])
        d_lb = self.dram_in("p_a_lb_fm", [2, 128, 16])
        d_bsb = self.dram_in("p_a_bs_bc", [2, 128, 8, 128])
        d_convw = self.dram_in("p_b_convw_fm", [2, 128, 32, 4])
        d_convb = self.dram_in("p_b_convb_fm", [2, 128, 32])
        d_alog = self.dram_in("p_b_alog_bc", [2, 128, 32])
        d_dtb = self.dram_in("p_b_dtb_bc", [2, 128, 32])
        d_dskip = self.dram_in("p_b_dskip_fm", [2, 128, 16])
        d_bnw = self.dram_in("p_b_nw_fm", [2, 128, 16])

        ps_t = nc.alloc_psum_tensor("ps", [128, 4096], F32)
        self.ps = ps_t.ap()

        sb = self.sb
        op = self.op
        self.ident_f = sb("ident_f", [128, 128])
        self.ident_b = sb("ident_b", [128, 128], BF16)
        self.tri_f = sb("tri_f", [128, 128])
        self.tri_b = sb("tri_b", [128, 128], BF16)
        self.striu_b = sb("striu_b", [128, 128], BF16)
        self.ones_b = sb("ones_b", [128, 128], BF16)
        self.normw = sb("normw", [128, 4, 8])
        self.d_fnw = d_fnw
        ctmp = sb("ctmp", [128, 128])

        def ld(dst, src, key, q="sp"):
            op(q, lambda e: e.dma_start(out=dst, in_=src), writes=[key], dma="c")

        ld(self.ident_f, d_ident, "ident_f")
        ld(self.tri_f, d_tri, "tri_f")
        ld(ctmp, d_striu, "ctmp")
        ld(self.normw, d_normw, "normw")
        op("dve", lambda e: e.tensor_copy(out=self.ident_b, in_=self.ident_f), reads=["ident_f"], writes=["ident_b"])
        op("dve", lambda e: e.tensor_copy(out=self.tri_b, in_=self.tri_f), reads=["tri_f"], writes=["tri_b"])
        op("dve", lambda e: e.tensor_copy(out=self.striu_b, in_=ctmp), reads=["ctmp"], writes=["striu_b"])
        op("dve", lambda e: e.memset(self.ones_b, 1.0), writes=["ones_b"])

        self.resid = sb("resid", [128, NCH, D_MODEL])
        self.hnT = sb("hnT", [128, 8, T], BF16)
        self.junk = sb("junk", [128, D_MODEL], BF16)
        self.yb = [sb("yb%d" % i, [128, 1024]) for i in range(2)]
        self.hs = self.yb[0]
        scr = sb("scr", [128, 3, 2048], BF16)
        self.xs = scr[:, 0, :].rearrange("p (h d) -> p h d", h=32)
        self.xdt = scr[:, 1, :].rearrange("p (h d) -> p h d", h=32)
        self.xw = scr[:, 2, :].rearrange("p (h d) -> p h d", h=32)
        self.zs_t = [scr[:, 0, i * 1024:(i + 1) * 1024].bitcast(F32) for i in range(2)]
        self.m2_t = [scr[:, 1, i * 1024:(i + 1) * 1024].bitcast(F32) for i in range(2)]
        self.g1_t = [scr[:, 2, i * 1024:(i + 1) * 1024].bitcast(F32) for i in range(2)]
        self.ss = sb("ss", [128, 8])
        self.rstd = sb("rstd", [128, 8])
        self.lntmp = sb("lntmp", [128, 8])
        self.ring = sb("ring", [128, RING, 8, 512], BF16)
        self.gT = sb("gT", [128, 16, T], BF16)
        self.big = sb("big", [128, NCH, D_INNER], BF16)

        self.wq = []
        self.wq_loaded = 0
        self.wq_used = 0

        kinds = ["A" if (li % 2 == 0) else "B" for li in self.layers]
        if "A" in kinds:
            self.setup_A(d_wsT, d_lw, d_lb, d_bsb)
        if "B" in kinds:
            self.setup_B(d_convw, d_convb, d_alog, d_dtb, d_dskip, d_bnw)

        for t in range(NT):
            for li in self.layers:
                if li % 2 == 0:
                    self.queue_A_weights(li // 2)
                else:
                    self.queue_B_weights(li // 2)

        self.blocks_per_tile = len(self.wq) // NT
        self.wscr = nc.dram_tensor("wscr", [self.blocks_per_tile, 128, 8, 512], BF16, kind="Internal").ap()
        xv = self.x.rearrange("(t c p) d -> t p c d", p=128, c=NCH)
        yv = self.y.rearrange("(t c p) d -> t p c d", p=128, c=NCH)
        out_keys = []
        self.xv, self.yv, self.out_keys = xv, yv, out_keys
        for c in range(NCH):
            self.load_x_chunk(0, c)
        for t in range(NT):
            self.cur_tile = t
            for idx, li in enumerate(self.layers):
                last = (idx + 1 == len(self.layers))
                self.next_li = None if last else self.layers[idx + 1]
                self.is_last = last
                self.do_final = last and self.final
                if li % 2 == 0:
                    self.layer_A(t, li)
                else:
                    self.layer_B(t, li)
            self.hn_ready = None
        op("sp", lambda e: None, reads=out_keys)
        self.S.ops[-1]["fn"] = lambda e: e.nop()
        self.S.emit()
        return nc

    def load_x_chunk(self, t, c):
        self.op("sp", lambda e, t=t, c=c: e.dma_start(out=self.resid[:, c, :], in_=self.xv[t][:, c, :]),
                writes=[("resid", c)], dma="x%d" % c)

    def store_y_chunk(self, t, c):
        self.op("sp", lambda e, t=t, c=c: e.dma_start(out=self.yv[t][:, c, :], in_=self.resid[:, c, :]),
                reads=[("resid", c)], writes=[("out", t, c)], dma="y%d" % c)
        self.out_keys.append(("out", t, c))
        if t + 1 < self.n_tiles:
            self.load_x_chunk(t + 1, c)

    def wslot(self, i):
        return self.ring[:, i % RING]

    def _emit_load(self, j):
        srcs = self.wq[j]
        slot = self.wslot(j)
        nb = self.blocks_per_tile
        tj, bj = j // nb, j % nb
        skey = ("ring", j % RING)
        stream = "w%d" % (j % RING)
        if tj == 0:
            def fn(e, srcs=srcs, slot=slot):
                res = []
                for (c0, c1, src) in srcs:
                    res.append(e.dma_start(out=slot[:, :, c0:c1], in_=src))
                return res
            self.op("pool", fn, writes=[skey], dma=stream, ndma=len(srcs))
            if self.n_tiles > 1:
                self.op("sp", lambda e, slot=slot, bj=bj: e.dma_start(out=self.wscr[bj], in_=slot),
                        reads=[skey], writes=[("wscr", bj)], dma="wb%d" % (j % RING))
        else:
            self.op("sp", lambda e, slot=slot, bj=bj: e.dma_start(out=slot, in_=self.wscr[bj]),
                    reads=[("wscr", bj)], writes=[skey], dma="h%d" % (j % RING))

    def use_block(self):
        i = self.wq_used
        self.wq_used += 1
        while self.wq_loaded < min(RING, len(self.wq)):
            self._emit_load(self.wq_loaded)
            self.wq_loaded += 1
        assert i < self.wq_loaded, (i, self.wq_loaded)
        return self.wslot(i), ("ring", i % RING), i

    def done_block(self, i):
        j = i + RING
        if j < len(self.wq):
            assert j == self.wq_loaded, (j, self.wq_loaded)
            self._emit_load(j)
            self.wq_loaded += 1

    def queue_A_weights(self, k):
        w_in = self.a_w_in[k].rearrange("(k p) n -> p k n", p=128)
        w_out = self.a_w_out[k]
        for j in range(4):
            c = 4096 + j * 512
            self.wq.append([(0, 512, w_in[:, :, c:c + 512])])
        for j in range(8):
            self.wq.append([(0, 256, w_in[:, :, j * 256:(j + 1) * 256]),
                            (256, 512, w_in[:, :, 2048 + j * 256:2048 + (j + 1) * 256])])
        self.queue_out_weights(w_out)

    def queue_out_weights(self, w_out):
        for ch in range(2):
            for kh in range(2):
                src = w_out[kh * 1024:(kh + 1) * 1024, ch * 512:(ch + 1) * 512].rearrange("(k p) n -> p k n", p=128)
                self.wq.append([(0, 512, src)])

    def queue_B_weights(self, k):
        w_in = self.b_w_in[k].rearrange("(k p) n -> p k n", p=128)
        for j in list(range(4, 12)) + list(range(4)):
            self.wq.append([(0, 512, w_in[:, :, j * 512:(j + 1) * 512])])
        self.queue_out_weights(self.b_w_out[k])

    def rmsnorm_to_hnT(self, li):
        if self.hn_ready == li:
            return
        for c in range(NCH):
            self.rmsnorm_chunk(li, c)

    def rmsnorm_chunk(self, li, c):
        self.rmsnorm_s1(li, c)
        self.rmsnorm_s2(li, c)

    def rmsnorm_s1(self, li, c):
        op = self.op
        if True:
            op("act", lambda e, c=c: e.activation(out=self.junk, in_=self.resid[:, c, :], func=AF.Square,
                                                  accum_out=self.ss[:, c:c + 1]),
               reads=[("resid", c)], writes=["junk", ("ss", c)] + [("junk", g_) for g_ in range(4)])
            self.rsqrt(self.rstd[:, c:c + 1], self.ss[:, c:c + 1], 1.0 / D_MODEL, NORM_EPS,
                       [("ss", c)], [("rstd", c)], self.lntmp[:, c:c + 1], ("lntmp", c))
            op("act", lambda e, c=c: e.activation(out=self.hs, in_=self.resid[:, c, :], func=AF.Copy,
                                                  scale=self.rstd[:, c:c + 1]),
               reads=[("resid", c), ("rstd", c)], writes=["yb0"])

    def rmsnorm_s2(self, li, c):
        op = self.op
        if True:
            for half in range(2):
                b = self.banks(1)

                def tr(e, b=b, half=half):
                    for j in range(4):
                        k = half * 4 + j
                        i = e.transpose(out=self.psf(b)[:, j * 128:(j + 1) * 128],
                                        in_=self.hs[:, k * 128:(k + 1) * 128], identity=self.ident_f)
                    return i
                op("pe", tr, reads=["yb0", "ident_f"], writes=self.bank_keys(b))

                def ev(e, b=b, half=half, c=c):
                    return e.tensor_tensor(
                        out=self.hnT[:, half * 4:half * 4 + 4, c * 128:(c + 1) * 128],
                        in0=self.psf(b).rearrange("p (k t) -> p k t", k=4),
                        in1=self.normw[:, li, half * 4:half * 4 + 4].unsqueeze(2).to_broadcast([128, 4, 128]),
                        op=ALU.mult)
                op("dve", ev, reads=self.bank_keys(b) + ["normw"], writes=[("hnT", c)])

    def out_proj(self, next_li=None, do_final=False):
        op = self.op
        if do_final:
            self.final_norm_begin()
        for ch in range(2):
            s0, k0, i0 = self.use_block()
            s1, k1, i1 = self.use_block()
            for c in range(NCH):
                b = self.banks(1)

                def mm(e, b=b, c=c, s0=s0, s1=s1):
                    for kh, s in ((0, s0), (1, s1)):
                        for k in range(8):
                            i = e.matmul(self.psf(b), lhsT=self.gT[:, kh * 8 + k, c * 128:(c + 1) * 128],
                                         rhs=s[:, k, :], start=(kh == 0 and k == 0), stop=(kh == 1 and k == 7))
                    return i
                op("pe", mm, reads=[k0, k1, ("gT", c)], writes=self.bank_keys(b))

                def add(e, b=b, c=c, ch=ch):
                    dst = self.resid[:, c, ch * 512:(ch + 1) * 512]
                    return e.tensor_tensor(out=dst, in0=self.psf(b), in1=dst, op=ALU.add)
                op("dve", add, reads=self.bank_keys(b) + [("resid", c)], writes=[("resid", c)])
                if ch == 1:
                    if next_li is not None:
                        if c > 0:
                            self.rmsnorm_s2(next_li, c - 1)
                        self.rmsnorm_s1(next_li, c)
                    else:
                        if do_final:
                            self.final_norm_chunk(c)
                        if self.is_last:
                            self.store_y_chunk(self.cur_tile, c)
            if ch == 1 and next_li is not None:
                self.rmsnorm_s2(next_li, NCH - 1)
            self.done_block(i0)
            self.done_block(i1)
        if next_li is not None:
            self.hn_ready = next_li

    def final_norm_begin(self):
        self.op("sp", lambda e: e.dma_start(out=self.yb[1], in_=self.d_fnw), writes=["yb1"], dma="fnw")

    def final_norm_chunk(self, c):
        op = self.op
        fnw = self.yb[1]
        op("act", lambda e, c=c: e.activation(out=self.junk, in_=self.resid[:, c, :], func=AF.Square,
                                              accum_out=self.ss[:, c:c + 1]),
           reads=[("resid", c)], writes=["junk", ("ss", c)] + [("junk", g_) for g_ in range(4)])
        self.rsqrt(self.rstd[:, c:c + 1], self.ss[:, c:c + 1], 1.0 / D_MODEL, NORM_EPS,
                   [("ss", c)], [("rstd", c)], self.lntmp[:, c:c + 1], ("lntmp", c))
        op("dve", lambda e, c=c: e.scalar_tensor_tensor(out=self.resid[:, c, :], in0=self.resid[:, c, :],
                                                        scalar=self.rstd[:, c:c + 1], in1=fnw,
                                                        op0=ALU.mult, op1=ALU.mult),
           reads=[("resid", c), ("rstd", c), "yb1"], writes=[("resid", c)])

    def setup_A(self, d_wsT, d_lw, d_lb, d_bsb):
        op = self.op
        sb = self.sb
        self.WcT = sb("WcT", [128, 2, 8, 128], BF16)
        self.lw = sb("lw", [128, 2, 16])
        self.lb = sb("lb", [128, 2, 16])
        self.bias2 = sb("bias2", [128, 2, 16, 128], BF16)
        self.st6 = sb("st6", [128, NCH, 4, 6])
        self.mv = sb("mv", [128, NCH, 2])
        self.vrstd = sb("vrstd", [128, NCH])
        self.vtmp = sb("vtmp", [128, NCH])
        bigf = self.big.rearrange("p c n -> p (c n)")
        wtmp = bigf[:, 0:2048].bitcast(F32).rearrange("p (g t) -> p g t", g=8)
        bsb = bigf[:, 2048:4096].bitcast(F32).rearrange("p (g t) -> p g t", g=8)
        bigk = [("big", c) for c in range(NCH)]
        for k in range(2):
            if (2 * k) not in self.layers:
                continue
            op("sp", lambda e, k=k: [e.dma_start(out=wtmp, in_=d_wsT[k]), e.dma_start(out=bsb, in_=d_bsb[k])],
               writes=bigk, dma="c", ndma=2)
            op("sp", lambda e, k=k: e.dma_start(out=self.lw[:, k, :], in_=d_lw[k]), writes=[("lw", k)], dma="c")
            op("sp", lambda e, k=k: e.dma_start(out=self.lb[:, k, :], in_=d_lb[k]), writes=[("lb", k)], dma="c")
            op("dve", lambda e, k=k: e.tensor_tensor(out=self.WcT[:, k], in0=wtmp,
                                                     in1=self.tri_f.unsqueeze(1).to_broadcast([128, 8, 128]),
                                                     op=ALU.mult),
               reads=bigk + ["tri_f"], writes=[("WcT", k)])
            b = self.banks(2)

            def rs(e, k=k, b=b):
                for g in range(8):
                    i = e.matmul(self.psf(b, 2)[:, g * 128:(g + 1) * 128], lhsT=self.ones_b,
                                 rhs=self.WcT[:, k, g, :], start=True, stop=True)
                return i
            op("pe", rs, reads=["ones_b", ("WcT", k)], writes=self.bank_keys(b, 2))
            for fb in range(16):
                g = fb // 2
                op("dve", lambda e, k=k, fb=fb, g=g, b=b: e.scalar_tensor_tensor(
                    out=self.bias2[:, k, fb, :], in0=self.psf(b, 2)[:, g * 128:(g + 1) * 128],
                    scalar=self.lb[:, k, fb:fb + 1], in1=bsb[:, g, :], op0=ALU.mult, op1=ALU.add),
                   reads=self.bank_keys(b, 2) + [("lb", k)] + bigk, writes=[("bias2", k)])

    def layer_A(self, t, li):
        op = self.op
        k = li // 2
        self.rmsnorm_to_hnT(li)
        vn = self.big
        for j in range(4):
            slot, skey, sidx = self.use_block()
            for c in range(NCH):
                b = self.banks(1)

                def mm(e, b=b, c=c, slot=slot):
                    for kk in range(8):
                        i = e.matmul(self.psf(b), lhsT=self.hnT[:, kk, c * 128:(c + 1) * 128],
                                     rhs=slot[:, kk, :], start=(kk == 0), stop=(kk == 7))
                    return i
                op("pe", mm, reads=[("hnT", c), skey], writes=self.bank_keys(b))
                op("act", lambda e, b=b, j=j, c=c: e.activation(out=vn[:, c, j * 512:(j + 1) * 512], in_=self.psf(b),
                                                                func=AF.Copy),
                   reads=self.bank_keys(b), writes=[("big", c), ("vcopy", b)])
                op("dve", lambda e, b=b, j=j, c=c: e.bn_stats(out=self.st6[:, c, j, :], in_=self.psf(b)),
                   reads=self.bank_keys(b) + [("vcopy", b)], writes=[("st6", c, j)])
            self.done_block(sidx)
        for c in range(NCH):
            op("dve", lambda e, c=c: e.bn_aggr(out=self.mv[:, c, :], in_=self.st6[:, c].rearrange("p j s -> p (j s)")),
               reads=[("st6", c, j) for j in range(4)], writes=[("mv", c)])
            self.rsqrt(self.vrstd[:, c:c + 1], self.mv[:, c, 1:2], 1.0, LN_EPS, [("mv", c)], [("vrstd", c)],
                       self.vtmp[:, c:c + 1], ("vtmp", c))
            op("dve", lambda e, c=c: e.tensor_scalar(
                out=vn[:, c, :], in0=vn[:, c, :], scalar1=self.mv[:, c, 0:1], scalar2=self.vrstd[:, c:c + 1],
                op0=ALU.subtract, op1=ALU.mult),
               reads=[("big", c), ("mv", c), ("vrstd", c)], writes=[("big", c)])
        for j in range(8):
            slot, skey, sidx = self.use_block()
            for fi in range(2):
                fb = 2 * j + fi
                g = fb // 2
                bz = self.banks(1)
                bu = self.banks(1)
                bm = self.banks(1)
                i2 = fb % 2

                def mmz(e, bz=bz, fi=fi, slot=slot):
                    for kk in range(8):
                        i = e.matmul(self.psf(bz), lhsT=slot[:, kk, fi * 128:(fi + 1) * 128],
                                     rhs=self.hnT[:, kk, :], start=(kk == 0), stop=(kk == 7))
                    return i
                op("pe", mmz, reads=[skey] + [("hnT", c) for c in range(NCH)], writes=self.bank_keys(bz))

                def mmu(e, bu=bu, fi=fi, slot=slot):
                    for kk in range(8):
                        i = e.matmul(self.psf(bu), lhsT=slot[:, kk, 256 + fi * 128:256 + (fi + 1) * 128],
                                     rhs=self.hnT[:, kk, :], start=(kk == 0), stop=(kk == 7))
                    return i
                op("pe", mmu, reads=[skey] + [("hnT", c) for c in range(NCH)], writes=self.bank_keys(bu))

                def mmm(e, bm=bm, fb=fb, g=g):
                    for c in range(NCH):
                        i = e.matmul(self.psf(bm)[:, c * 128:(c + 1) * 128],
                                     lhsT=vn[:, c, fb * 128:(fb + 1) * 128],
                                     rhs=self.WcT[:, k, g, :], start=True, stop=True)
                    return i
                op("pe", mmm, reads=[("big", c) for c in range(NCH)] + [("WcT", k)], writes=self.bank_keys(bm))

                zs = self.zs_t[i2]
                m2 = self.m2_t[i2]
                g1 = self.g1_t[i2]
                op("act", lambda e, bz=bz, zs=zs: e.activation(out=zs, in_=self.psf(bz), func=AF.Silu),
                   reads=self.bank_keys(bz), writes=[("X0", i2)])
                op("dve", lambda e, bm=bm, fb=fb, m2=m2: e.scalar_tensor_tensor(
                    out=m2.rearrange("p (c t) -> p c t", c=NCH),
                    in0=self.psf(bm).rearrange("p (c t) -> p c t", c=NCH),
                    scalar=self.lw[:, k, fb:fb + 1],
                    in1=self.bias2[:, k, fb, :].unsqueeze(1).to_broadcast([128, NCH, 128]),
                    op0=ALU.mult, op1=ALU.add),
                   reads=self.bank_keys(bm) + [("lw", k), ("bias2", k)], writes=[("X1", i2)])
                op("dve", lambda e, bu=bu, zs=zs, g1=g1: e.tensor_tensor(out=g1, in0=self.psf(bu), in1=zs, op=ALU.mult),
                   reads=self.bank_keys(bu) + [("X0", i2)], writes=[("X2", i2)])
                op("dve", lambda e, fb=fb, g1=g1, m2=m2: e.tensor_tensor(out=self.gT[:, fb, :], in0=g1, in1=m2, op=ALU.mult),
                   reads=[("X2", i2), ("X1", i2)], writes=[("gT", c) for c in range(NCH)])
            self.done_block(sidx)
        self.out_proj(self.next_li, self.do_final)

    def setup_B(self, d_convw, d_convb, d_alog, d_dtb, d_dskip, d_bnw):
        op = self.op
        sb = self.sb
        self.convw = sb("convw", [128, 2, 32, 4])
        self.convb = sb("convb", [128, 2, 32])
        self.Abc = sb("Abc", [128, 2, 32])
        self.dtb = sb("dtb", [128, 2, 32])
        self.dsk = sb("dsk", [128, 2, 16])
        self.bnw = sb("bnw", [128, 2, 16])
        self.Ddiag = sb("Ddiag", [128, 2, 16, 128], BF16)
        self.state = sb("state", [128, 2, D_INNER])
        self.carry = sb("carry", [128, 2, 32, 3], BF16)
        self.prev_b = sb("prev_b", [128, D_INNER], BF16)
        self.wdt = sb("wdt", [128, 8, 32], BF16)
        self.dtr = sb("dtr", [128, NCH, 32])
        self.dte1 = sb("dte1", [128, NCH, 32])
        self.dt = sb("dt", [128, NCH, 32])
        self.a = sb("a_dt", [128, NCH, 32])
        self.a_hi = sb("a_hi", [128, NCH, 32], BF16)
        self.a_lo = sb("a_lo", [128, NCH, 32], BF16)
        self.xraw = [sb("xraw%d" % i, [128, T + 4], BF16) for i in range(2)]
        self.dg = [sb("dg%d" % i, [128, 4, 128], BF16) for i in range(2)]
        self.xT = sb("xT", [128, 16, T], BF16)
        self.BT = sb("BT", [128, 8, T], BF16)
        self.CT = sb("CT", [128, 8, T], BF16)
        self.rb = [sb("rb%d" % i, [128, 4, 128], BF16) for i in range(3)]
        self.Eb = [sb("Eb%d" % i, [128, 4, 128], BF16) for i in range(3)]
        self.wp = [sb("wp%d" % i, [128, 4, 128], BF16) for i in range(3)]
        self.cbm = sb("cbm", [128, 8, 128], BF16)
        self.sc2 = sb("sc2", [128, 32])
        self.ecd = sb("ecd", [128, 96])
        self.Btok = sb("Btok", [128, 8, 128], BF16)
        self.gss = sb("gss", [128, 8])
        self.grstd = sb("grstd", [128, 8])
        self.gtmp = sb("gtmp", [128, 8])
        self.gn = sb("gn", [128, D_INNER], BF16)
        for k in range(2):
            if (2 * k + 1) not in self.layers:
                continue
            op("sp", lambda e, k=k: e.dma_start(out=self.convw[:, k], in_=d_convw[k]), writes=[("convw", k)], dma="c")
            op("sp", lambda e, k=k: e.dma_start(out=self.convb[:, k], in_=d_convb[k]), writes=[("convb", k)], dma="c")
            op("sp", lambda e, k=k: e.dma_start(out=self.Abc[:, k], in_=d_alog[k]), writes=[("Abc", k)], dma="c")
            op("sp", lambda e, k=k: e.dma_start(out=self.dtb[:, k], in_=d_dtb[k]), writes=[("dtb", k)], dma="c")
            op("sp", lambda e, k=k: e.dma_start(out=self.dsk[:, k], in_=d_dskip[k]), writes=[("dsk", k)], dma="c")
            op("sp", lambda e, k=k: e.dma_start(out=self.bnw[:, k], in_=d_bnw[k]), writes=[("bnw", k)], dma="c")
            op("act", lambda e, k=k: e.activation(out=self.Abc[:, k], in_=self.Abc[:, k], func=AF.Exp),
               reads=[("Abc", k)], writes=[("Abc", k)])
            op("dve", lambda e, k=k: e.tensor_scalar(out=self.Abc[:, k], in0=self.Abc[:, k], scalar1=-1.0,
                                                     scalar2=None, op0=ALU.mult),
               reads=[("Abc", k)], writes=[("Abc", k)])
            for fb in range(16):
                op("dve", lambda e, k=k, fb=fb: e.tensor_scalar(out=self.Ddiag[:, k, fb, :], in0=self.ident_f,
                                                               scalar1=self.dsk[:, k, fb:fb + 1], scalar2=None,
                                                               op0=ALU.mult),
                   reads=["ident_f", ("dsk", k)], writes=[("Ddiag", k)])
            op("dve", lambda e, k=k: e.memset(self.state[:, k], 0.0), writes=[("state", k, 0), ("state", k, 1)])
            op("dve", lambda e, k=k: e.memset(self.carry[:, k], 0.0), writes=[("carry", k, cb) for cb in range(32)])

    def layer_B(self, t, li):
        op = self.op
        k = li // 2
        allc = list(range(NCH))
        self.rmsnorm_to_hnT(li)
        op("act", lambda e: e.activation(out=self.prev_b, in_=self.state[:, k], func=AF.Copy),
           reads=[("state", k, 0), ("state", k, 1)], writes=[("prev_b", 0), ("prev_b", 1)])
        wdt_src = self.b_w_in[k].rearrange("(k p) n -> p k n", p=128)[:, :, 6144:6176]
        op("pool", lambda e: e.dma_start(out=self.wdt, in_=wdt_src), writes=["wdt"], dma="wdt")
        b = self.banks(1)

        def mmdt(e, b=b):
            for c in range(NCH):
                for kk in range(8):
                    i = e.matmul(self.psf(b)[:, c * 32:(c + 1) * 32], lhsT=self.hnT[:, kk, c * 128:(c + 1) * 128],
                                 rhs=self.wdt[:, kk, :], start=(kk == 0), stop=(kk == 7))
            return i
        op("pe", mmdt, reads=["wdt"] + [("hnT", c) for c in allc], writes=self.bank_keys(b))
        op("dve", lambda e, b=b: e.tensor_tensor(out=self.dtr, in0=self.psf(b)[:, 0:NCH * 32].rearrange("p (c h) -> p c h", c=NCH),
                                                 in1=self.dtb[:, k, :].unsqueeze(1).to_broadcast([128, NCH, 32]), op=ALU.add),
           reads=self.bank_keys(b) + [("dtb", k)], writes=["dtr"])
        op("act", lambda e: e.activation(out=self.dte1, in_=self.dtr, func=AF.Exp), reads=["dtr"], writes=["dte1"])
        op("act", lambda e: e.activation(out=self.dt, in_=self.dte1, func=AF.Ln, bias=1.0, scale=1.0),
           reads=["dte1"], writes=["dt"])
        op("dve", lambda e: e.tensor_tensor(out=self.a, in0=self.dt,
                                            in1=self.Abc[:, k, :].unsqueeze(1).to_broadcast([128, NCH, 32]), op=ALU.mult),
           reads=["dt", ("Abc", k)], writes=["a"])
        op("dve", lambda e: e.tensor_copy(out=self.a_hi, in_=self.a), reads=["a"], writes=["a_hi"])
        op("dve", lambda e: e.tensor_tensor(out=self.a_lo, in0=self.a, in1=self.a_hi, op=ALU.subtract),
           reads=["a", "a_hi"], writes=["a_lo"])
        def z_unit(j):
            slot, skey, sidx = self.use_block()
            for c in range(NCH):
                b = self.banks(1)

                def mm(e, b=b, c=c, slot=slot):
                    for kk in range(8):
                        i = e.matmul(self.psf(b), lhsT=self.hnT[:, kk, c * 128:(c + 1) * 128],
                                     rhs=slot[:, kk, :], start=(kk == 0), stop=(kk == 7))
                    return i
                op("pe", mm, reads=[skey, ("hnT", c)], writes=self.bank_keys(b))
                op("act", lambda e, b=b, c=c, j=j: e.activation(out=self.big[:, c, j * 512:(j + 1) * 512],
                                                                in_=self.psf(b), func=AF.Silu),
                   reads=self.bank_keys(b), writes=[("big", c, j // 2)] + ([("big", c)] if j % 2 == 0 else []))
            self.done_block(sidx)

        pending = []

        def conv_part(cb, b1):
            i2 = cb % 2
            xr = self.xraw[i2]
            dg = self.dg[i2]
            op("act", lambda e, xr=xr, cb=cb: e.activation(out=xr[:, 0:3], in_=self.carry[:, k, cb, :], func=AF.Copy),
               reads=[("carry", k, cb)], writes=[("xraw", i2, "c")])
            op("act", lambda e, xr=xr, b1=b1: e.activation(out=xr[:, 3:3 + T], in_=self.psf(b1), func=AF.Copy),
               reads=self.bank_keys(b1), writes=[("xraw", i2, "m")])
            op("act", lambda e, xr=xr, cb=cb: e.activation(out=self.carry[:, k, cb, :], in_=xr[:, T:T + 3], func=AF.Copy),
               reads=[("xraw", i2, "m")], writes=[("carry", k, cb)])
            for tap in range(4):
                op("dve", lambda e, dg=dg, tap=tap, cb=cb: e.tensor_scalar(
                    out=dg[:, tap, :], in0=self.ident_b, scalar1=self.convw[:, k, cb, tap:tap + 1],
                    scalar2=None, op0=ALU.mult),
                   reads=["ident_b", ("convw", k)], writes=[("dg", i2, tap)])
            b2 = self.banks(1)

            def mmc(e, b2=b2, xr=xr, dg=dg):
                for tap in range(4):
                    i = e.matmul(self.psf(b2), lhsT=dg[:, tap, :], rhs=xr[:, tap:tap + T],
                                 start=(tap == 0), stop=(tap == 3))
                return i
            op("pe", mmc, reads=[("xraw", i2, "c"), ("xraw", i2, "m")] + [("dg", i2, tap) for tap in range(4)], writes=self.bank_keys(b2))
            if cb < 16:
                dst, dkey = self.xT[:, cb, :], "xT"
            elif cb < 24:
                dst, dkey = self.BT[:, cb - 16, :], "BT"
            else:
                dst, dkey = self.CT[:, cb - 24, :], "CT"
            op("act", lambda e, b2=b2, dst=dst, cb=cb: e.activation(out=dst, in_=self.psf(b2), func=AF.Silu,
                                                                    bias=self.convb[:, k, cb:cb + 1]),
               reads=self.bank_keys(b2) + [("convb", k)], writes=[dkey])

        for j in range(8):
            slot, skey, sidx = self.use_block()
            for q in range(4):
                cb = 4 * j + q
                b1 = self.banks(1)

                def mm(e, b1=b1, q=q, slot=slot):
                    for kk in range(8):
                        i = e.matmul(self.psf(b1), lhsT=slot[:, kk, q * 128:(q + 1) * 128],
                                     rhs=self.hnT[:, kk, :], start=(kk == 0), stop=(kk == 7))
                    return i
                op("pe", mm, reads=[skey] + [("hnT", c) for c in allc], writes=self.bank_keys(b1))
                if pending:
                    conv_part(*pending.pop())
                pending.append((cb, b1))
            self.done_block(sidx)
        conv_part(*pending.pop())
        def head(c):
            cc = slice(c * 128, (c + 1) * 128)
            bcb = self.banks(2)

            def mmcb(e, bcb=bcb, cc=cc):
                for g in range(8):
                    i = e.matmul(self.psf(bcb, 2)[:, g * 128:(g + 1) * 128], lhsT=self.BT[:, g, cc],
                                 rhs=self.CT[:, g, cc], start=True, stop=True)
                return i
            op("pe", mmcb, reads=["BT", "CT"], writes=self.bank_keys(bcb, 2))
            op("dve", lambda e, bcb=bcb: e.tensor_tensor(out=self.cbm, in0=self.psf(bcb, 2).rearrange("p (g l) -> p g l", g=8),
                                                         in1=self.tri_b.unsqueeze(1).to_broadcast([128, 8, 128]), op=ALU.mult),
               reads=self.bank_keys(bcb, 2) + ["tri_b"], writes=["cbm"])
            bx = self.banks(2)

            def trx(e, bx=bx, cc=cc):
                for fb in range(16):
                    i = e.transpose(out=self.psb(bx, 2)[:, fb * 128:(fb + 1) * 128], in_=self.xT[:, fb, cc],
                                    identity=self.ident_b)
                return i
            op("pe", trx, reads=["xT", "ident_b"], writes=self.bank_keys(bx, 2))
            bB = self.banks(1)

            def trb(e, bB=bB, cc=cc):
                for g in range(8):
                    i = e.transpose(out=self.psb(bB)[:, g * 128:(g + 1) * 128], in_=self.BT[:, g, cc],
                                    identity=self.ident_b)
                return i
            op("pe", trb, reads=["BT", "ident_b"], writes=self.bank_keys(bB))
            op("act", lambda e, bB=bB: e.activation(out=self.Btok.rearrange("p g n -> p (g n)"), in_=self.psb(bB), func=AF.Copy),
               reads=self.bank_keys(bB), writes=["Btok"])
            bcs = self.banks(1)

            def mmcs(e, bcs=bcs, c=c):
                first = True
                for (lo, lt) in ((0, self.tri_b), (32, self.ones_b), (64, self.striu_b)):
                    for av in (self.a_hi, self.a_lo):
                        i = e.matmul(self.psf(bcs)[:, lo:lo + 32], lhsT=lt, rhs=av[:, c, :], start=first,
                                     stop=(lo == 64 and av is self.a_lo))
                        first = False
                return i
            op("pe", mmcs, reads=["tri_b", "ones_b", "striu_b", "a_hi", "a_lo"], writes=self.bank_keys(bcs))
            op("act", lambda e, bcs=bcs: e.activation(out=self.ecd, in_=self.psf(bcs)[:, 0:96], func=AF.Exp),
               reads=self.bank_keys(bcs), writes=["ecd"])
            op("dve", lambda e, c=c: e.tensor_tensor(out=self.sc2, in0=self.ecd[:, 64:96], in1=self.dt[:, c, :], op=ALU.mult),
               reads=["ecd", "dt"], writes=["sc2"])
            for hf in range(2):
                hh = slice(16 * hf, 16 * hf + 16)
                op("dve", lambda e, c=c, hh=hh, hf=hf, bx=bx: e.tensor_tensor(
                    out=self.xdt[:, hh, :],
                    in0=self.psb(bx, 2)[:, hf * 1024:(hf + 1) * 1024].rearrange("p (h d) -> p h d", h=16),
                    in1=self.dt[:, c, hh].unsqueeze(2).to_broadcast([128, 16, 64]), op=ALU.mult),
                   reads=self.bank_keys(bx, 2) + ["dt"], writes=[("X1", hf)])
            for hf in range(2):
                hh = slice(16 * hf, 16 * hf + 16)
                op("dve", lambda e, hh=hh, hf=hf, bx=bx: e.tensor_tensor(
                    out=self.xw[:, hh, :],
                    in0=self.psb(bx, 2)[:, hf * 1024:(hf + 1) * 1024].rearrange("p (h d) -> p h d", h=16),
                    in1=self.sc2[:, hh].unsqueeze(2).to_broadcast([128, 16, 64]), op=ALU.mult),
                   reads=self.bank_keys(bx, 2) + ["sc2"], writes=[("X2", hf)])

        def groups(c, prev=None):
            cc = slice(c * 128, (c + 1) * 128)
            half_banks = {}

            def stageA(g, c=c):
                i3 = g % 3
                rb, Eb = self.rb[i3], self.Eb[i3]
                for jh in range(4):
                    h = 4 * g + jh
                    if jh < 3:
                        op("dve", lambda e, rb=rb, jh=jh, h=h: e.tensor_scalar(
                            out=rb[:, jh, :], in0=self.tri_b, scalar1=self.a[:, c, h:h + 1], scalar2=None, op0=ALU.mult),
                           reads=["tri_b", "a"], writes=[("rb", i3, jh)])
                    else:
                        op("act", lambda e, rb=rb, jh=jh, h=h: e.activation(
                            out=rb[:, jh, :], in_=self.tri_b, func=AF.Copy, scale=self.a[:, c, h:h + 1]),
                           reads=["tri_b", "a"], writes=[("rb", i3, jh)])

            def stageA2(g, c=c):
                i3 = g % 3
                rb, Eb = self.rb[i3], self.Eb[i3]
                bD = self.banks(1)
                op("pe", lambda e, bD=bD, rb=rb: e.matmul(self.psf(bD), lhsT=self.striu_b,
                                                          rhs=rb.rearrange("p j l -> p (j l)"), start=True, stop=True),
                   reads=["striu_b"] + [("rb", i3, jh) for jh in range(4)], writes=self.bank_keys(bD))
                op("act", lambda e, bD=bD, Eb=Eb: e.activation(out=Eb.rearrange("p j l -> p (j l)"), in_=self.psf(bD), func=AF.Exp),
                   reads=self.bank_keys(bD), writes=[("Eb", i3)])

            def stageB(g, c=c, cc=cc):
                i3 = g % 3
                Eb, wp = self.Eb[i3], self.wp[i3]
                pass

            def stageB1(g, c=c):
                i3 = g % 3
                Eb, wp = self.Eb[i3], self.wp[i3]
                op("dve", lambda e, Eb=Eb, wp=wp, g=g: e.tensor_tensor(
                    out=wp, in0=Eb, in1=self.cbm[:, g, :].unsqueeze(1).to_broadcast([128, 4, 128]), op=ALU.mult),
                   reads=[("Eb", i3), "cbm"], writes=[("wp", i3)])

            def stageB2(g, c=c, cc=cc):
                i3 = g % 3
                wp = self.wp[i3]
                hf, gl = g // 4, g % 4
                if gl == 0:
                    half_banks[hf] = (self.banks(2), self.banks(2))
                bA, bG = half_banks[hf]
                yb = self.yb[hf]

                def mmy(e, bA=bA, gl=gl, g=g, wp=wp):
                    for fi in range(2):
                        fb = 2 * g + fi
                        col = (2 * gl + fi) * 128
                        i = e.matmul(self.psf(bA, 2)[:, col:col + 128], lhsT=self.xT[:, fb, cc],
                                     rhs=self.Ddiag[:, k, fb, :], start=(gl % 2 == 0 and fi == 0), stop=False)
                    for jh in range(4):
                        h = 4 * g + jh
                        col = (4 * gl + jh) * 64
                        i = e.matmul(self.psf(bA, 2)[:, col:col + 64], lhsT=wp[:, jh, :],
                                     rhs=self.xdt[:, h, :], start=False, stop=False)
                    return i
                op("pe", mmy, reads=["xT", ("Ddiag", k), ("wp", i3), ("X1", hf)], writes=self.bank_keys(bA, 2))
                op("pe", lambda e, bG=bG, gl=gl, g=g: e.matmul(
                    self.psf(bG, 2)[:, gl * 256:(gl + 1) * 256], lhsT=self.CT[:, g, cc],
                    rhs=self.prev_b[:, g * 256:(g + 1) * 256], start=True, stop=True),
                   reads=["CT", ("prev_b", hf)], writes=self.bank_keys(bG, 2))
                if gl != 3:
                    return
                tb = self.xs[:, 16 * hf:16 * hf + 16, :]
                op("dve", lambda e, bG=bG, tb=tb, hf=hf: e.tensor_tensor(
                    out=tb, in0=self.psf(bG, 2).rearrange("p (h d) -> p h d", h=16),
                    in1=self.ecd[:, 16 * hf:16 * hf + 16].unsqueeze(2).to_broadcast([128, 16, 64]), op=ALU.mult),
                   reads=self.bank_keys(bG, 2) + ["ecd"], writes=[("X0", hf)])

                def mmadd(e, bA=bA, tb=tb):
                    tf = tb.rearrange("p h d -> p (h d)")
                    for i in range(2):
                        ins = e.matmul(self.psf(bA + i), lhsT=self.ident_b, rhs=tf[:, i * 512:(i + 1) * 512],
                                       start=False, stop=True)
                    return ins
                op("pe", mmadd, reads=["ident_b", ("X0", hf)], writes=self.bank_keys(bA, 2))
                op("dve", lambda e, bA=bA, yb=yb, hf=hf: e.tensor_tensor(
                    out=yb, in0=self.psf(bA, 2), in1=self.big[:, c, hf * 1024:(hf + 1) * 1024], op=ALU.mult),
                   reads=self.bank_keys(bA, 2) + [("big", c, hf)], writes=["yb%d" % hf])
                for gl2 in range(4):
                    g2 = 4 * hf + gl2
                    op("act", lambda e, yb=yb, gl2=gl2, g2=g2: e.activation(out=self.junk[:, gl2 * 256:(gl2 + 1) * 256], in_=yb[:, gl2 * 256:(gl2 + 1) * 256],
                                                                          func=AF.Square, accum_out=self.gss[:, g2:g2 + 1]),
                       reads=["yb%d" % hf], writes=[("junk", gl2), ("gss", hf, gl2)])

            for step in range(-3, 8):
                if 0 <= step + 3 < 8:
                    stageA(step + 3)
                if 0 <= step + 2 < 8:
                    stageA2(step + 2)
                if 0 <= step + 1 < 8:
                    stageB1(step + 1)
                if 0 <= step < 8:
                    stageB2(step)
                if prev is not None:
                    if step == 1:
                        tailG1(prev)
                    elif step == 3:
                        tailG2(prev)
                    elif step == 4:
                        tailG3(prev)
                elif c == 0:
                    if step == -2:
                        z_unit(1)
                    elif step == 1:
                        z_unit(2)
                    elif step == 4:
                        z_unit(3)

        def tailG1(c):
            self.rsqrt(self.grstd, self.gss, 1.0 / 256.0, NORM_EPS, [("gss", h_, g_) for h_ in range(2) for g_ in range(4)], ["grstd"], self.gtmp, "gtmp")
            for g in range(8):
                hf, gl = g // 4, g % 4
                op("act", lambda e, g=g, hf=hf, gl=gl: e.activation(out=self.gn[:, g * 256:(g + 1) * 256],
                                                                    in_=self.yb[hf][:, gl * 256:(gl + 1) * 256],
                                                                    func=AF.Copy, scale=self.grstd[:, g:g + 1]),
                   reads=["yb%d" % hf, "grstd"], writes=[("gn", g)])

        tg_bank = {}

        def tailG2(c):
            bT = self.banks(2)
            tg_bank[c] = bT

            def trg(e, bT=bT):
                for fb in range(16):
                    i = e.transpose(out=self.psb(bT, 2)[:, fb * 128:(fb + 1) * 128], in_=self.gn[:, fb * 128:(fb + 1) * 128],
                                    identity=self.ident_b)
                return i
            op("pe", trg, reads=[("gn", g_) for g_ in range(8)] + ["ident_b"], writes=self.bank_keys(bT, 2))

        def tailG3(c):
            cc = slice(c * 128, (c + 1) * 128)
            bT = tg_bank[c]
            op("dve", lambda e, bT=bT, cc=cc: e.tensor_tensor(
                out=self.gT[:, :, cc], in0=self.psb(bT, 2).rearrange("p (f t) -> p f t", f=16),
                in1=self.bnw[:, k, :].unsqueeze(2).to_broadcast([128, 16, 128]), op=ALU.mult),
               reads=self.bank_keys(bT, 2) + [("bnw", k)], writes=[("gT", c)])

        def tailS(c):
            cc = slice(c * 128, (c + 1) * 128)
            bS = self.banks(4)

            def mms(e, bS=bS):
                for g in range(8):
                    i = e.matmul(self.psf(bS, 4)[:, g * 256:(g + 1) * 256], lhsT=self.Btok[:, g, :],
                                 rhs=self.xw[:, 4 * g:4 * g + 4, :].rearrange("p h d -> p (h d)"), start=True, stop=True)
                return i
            op("pe", mms, reads=["Btok", ("X2", 0), ("X2", 1)], writes=self.bank_keys(bS, 4))
            for hf in range(2):
                hs_ = slice(hf * 1024, (hf + 1) * 1024)
                op("pool", lambda e, hf=hf, hs_=hs_: e.tensor_tensor(
                    out=self.state[:, k, hs_].rearrange("p (h d) -> p h d", h=16),
                    in0=self.state[:, k, hs_].rearrange("p (h d) -> p h d", h=16),
                    in1=self.ecd[:, 32 + 16 * hf:48 + 16 * hf].unsqueeze(2).to_broadcast([128, 16, 64]), op=ALU.mult),
                   reads=[("state", k, hf), "ecd", ("prev_b", hf)], writes=[("state", k, hf)])
                op("dve", lambda e, bS=bS, hs_=hs_: e.tensor_tensor(out=self.state[:, k, hs_], in0=self.psf(bS, 4)[:, hs_],
                                                                  in1=self.state[:, k, hs_], op=ALU.add),
                   reads=self.bank_keys(bS, 4) + [("state", k, hf)], writes=[("state", k, hf)])
                op("act", lambda e, hs_=hs_: e.activation(out=self.prev_b[:, hs_], in_=self.state[:, k, hs_], func=AF.Copy),
                   reads=[("state", k, hf)], writes=[("prev_b", hf)])

        head(0)
        z_unit(0)
        for c in range(NCH):
            groups(c, prev=(c - 1 if c > 0 else None))
            tailS(c)
            if c + 1 < NCH:
                head(c + 1)
        tailG1(NCH - 1)
        tailG2(NCH - 1)
        tailG3(NCH - 1)
        self.out_proj(self.next_li, self.do_final)


def _host_params(p):
    f = np.float32
    out = {}
    k = np.arange(128)
    out["c_ident"] = np.eye(128, dtype=f)
    out["c_tri"] = (k[:, None] <= k[None, :]).astype(f)
    out["c_striu"] = (k[:, None] > k[None, :]).astype(f)
    out["p_normw_fm"] = np.ascontiguousarray(p["norm_w"].reshape(4, 8, 128).transpose(2, 0, 1)).astype(f)
    out["p_fnw_bc"] = np.ascontiguousarray(np.broadcast_to(p["final_norm_w"][None, :], (128, D_MODEL))).astype(f)
    out["p_a_wsT"] = np.ascontiguousarray(p["a_w_s"].transpose(0, 3, 1, 2)).astype(f)
    out["p_a_lw_fm"] = np.ascontiguousarray(p["a_ln_w"].reshape(2, 16, 128).transpose(0, 2, 1)).astype(f)
    out["p_a_lb_fm"] = np.ascontiguousarray(p["a_ln_b"].reshape(2, 16, 128).transpose(0, 2, 1)).astype(f)
    out["p_a_bs_bc"] = np.ascontiguousarray(np.broadcast_to(p["a_b_s"][:, None, :, :], (2, 128, 8, 128))).astype(f)
    out["p_b_convw_fm"] = np.ascontiguousarray(p["b_conv_w"].reshape(2, 4, 32, 128).transpose(0, 3, 2, 1)).astype(f)
    out["p_b_convb_fm"] = np.ascontiguousarray(p["b_conv_b"].reshape(2, 32, 128).transpose(0, 2, 1)).astype(f)
    out["p_b_alog_bc"] = np.ascontiguousarray(np.broadcast_to(p["b_a_log"][:, None, :], (2, 128, 32))).astype(f)
    out["p_b_dtb_bc"] = np.ascontiguousarray(np.broadcast_to(p["b_dt_bias"][:, None, :], (2, 128, 32))).astype(f)
    dsk = np.repeat(p["b_d_skip"], 64, axis=1)
    out["p_b_dskip_fm"] = np.ascontiguousarray(dsk.reshape(2, 16, 128).transpose(0, 2, 1)).astype(f)
    out["p_b_nw_fm"] = np.ascontiguousarray(p["b_norm_w"].reshape(2, 16, 128).transpose(0, 2, 1)).astype(f)
    for n in ("a_w_in", "a_w_out", "b_w_in", "b_w_out"):
        out[n] = np.ascontiguousarray(p[n]).astype(f)
    return out


_NC_CACHE = {}


def run_layers(x, params, layers, final, n_cores=None):
    B, L, _ = x.shape
    n_tiles = L // T
    key = (tuple(layers), bool(final), n_tiles)
    if key not in _NC_CACHE:
        _NC_CACHE[key] = Builder(layers, final, n_tiles).build()
    nc = _NC_CACHE[key]
    hp = _host_params(params)
    in_maps = []
    for b in range(B):
        m = dict(hp)
        m["x"] = np.ascontiguousarray(x[b]).astype(np.float32)
        in_maps.append(m)
    res = run_bass_kernel_spmd(nc, in_maps, core_ids=list(range(B)))
    return np.stack([r["y"] for r in res.results], axis=0)


def kernel(**inputs):
    x = np.asarray(inputs["x"], dtype=np.float32)
    params = {k: np.asarray(v, dtype=np.float32) for k, v in inputs.items() if k != "x"}
    return run_layers(x, params, list(range(DEPTH)), True)
```
